# Optimizing a Trainium2 kernel written in Bass

```python
import jax, jax.numpy as jnp
from jax import lax
import numpy as np

D_MODEL = 1024
BATCH = 8
SEQ = 2048
DEPTH = 4
DEC_BATCH = 32
DEC_SEQ = 16
PAST_LEN = 2048

CHUNK = 64
MIX_WIDTH = D_MODEL
M_WIDTH = MIX_WIDTH // 2
S_WIDTH = MIX_WIDTH - M_WIDTH
M_HEADS = 4
M_DK = M_WIDTH // M_HEADS
M_DV = M_DK
S_GROUPS = 4
CONV_W = 3
D_FF = 4 * D_MODEL
N_MOD = 6
EPS = 1e-6

OFF_Q = 0
OFF_K = OFF_Q + M_WIDTH
OFF_V = OFF_K + M_WIDTH
OFF_O = OFF_V + M_WIDTH
OFF_I = OFF_O + M_WIDTH
OFF_F = OFF_I + M_HEADS
OFF_B = OFF_F + M_HEADS
OFF_C = OFF_B + S_WIDTH
OFF_X = OFF_C + S_WIDTH
N_IN = OFF_X + S_WIDTH

kernel_name = "hybrid_mlstm_shortconv_streaming_step"


def _rms(x):
    x32 = x.astype(jnp.float32)
    return (x32 * lax.rsqrt(jnp.mean(jnp.square(x32), axis=-1, keepdims=True) + EPS)).astype(x.dtype)


def _group_rms(x, groups):
    shp = x.shape
    xg = x.reshape(shp[:-1] + (groups, shp[-1] // groups))
    return _rms(xg).reshape(shp)


def _mlstm_chunk(carry, inp):
    C0, n0, m0 = carry
    q, k, v, logi, logf = inp
    L = q.shape[2]
    b = jnp.cumsum(logf, axis=-1)
    a = logi - b
    m = b + jnp.maximum(m0[..., None], lax.cummax(a, axis=2))
    causal = jnp.tril(jnp.ones((L, L), dtype=bool))
    logw = (b - m)[..., :, None] + a[..., None, :]
    w = jnp.exp(jnp.where(causal, logw, -jnp.inf))
    g = jnp.exp(b + m0[..., None] - m)
    s = jnp.einsum('bhtd,bhsd->bhts', q, k) * w
    num = g[..., None] * jnp.einsum('bhtd,bhde->bhte', q, C0) + jnp.einsum('bhts,bhse->bhte', s, v)
    den = g * jnp.einsum('bhtd,bhd->bht', q, n0) + jnp.sum(s, axis=-1)
    h = num / jnp.maximum(jnp.abs(den), jnp.exp(-m))[..., None]
    mL = m[..., -1]
    wend = jnp.exp(b[..., -1:] - mL[..., None] + a)
    decay = jnp.exp(b[..., -1] + m0 - mL)
    C1 = decay[..., None, None] * C0 + jnp.einsum('bhs,bhsd,bhse->bhde', wend, k, v)
    n1 = decay[..., None] * n0 + jnp.einsum('bhs,bhsd->bhd', wend, k)
    return (C1, n1, mL), h


def _mlstm(q, k, v, logi, logf, C0, n0, m0):
    Bsz, T, H = q.shape[0], q.shape[1], q.shape[2]
    L = CHUNK if T % CHUNK == 0 else T
    NC = T // L
    f32 = jnp.float32

    def blocks(t):
        t = t.astype(f32).reshape((Bsz, NC, L, H) + t.shape[3:])
        return jnp.moveaxis(t, (1, 3), (0, 2))

    carry0 = (C0.astype(f32), n0.astype(f32), m0.astype(f32))
    carry, h = lax.scan(_mlstm_chunk, carry0, (blocks(q), blocks(k), blocks(v), blocks(logi), blocks(logf)))
    h = jnp.moveaxis(h, (0, 2), (1, 3)).reshape(Bsz, T, H, M_DV)
    return h, carry


def _layer(x, c, C0, n0, m0, conv_prev, w_ada, b_ada, g1, w_in, b_in, conv_w, g_mix, w_out, g2, w_up, w_down):
    Bsz, T, _ = x.shape
    mod = jax.nn.silu(c) @ w_ada + b_ada
    sh1, sc1, gt1, sh2, sc2, gt2 = jnp.split(mod[:, None, :], N_MOD, axis=-1)
    h = _rms(x) * g1 * (1 + sc1) + sh1
    z = h @ w_in + b_in
    q = z[..., OFF_Q:OFF_K].reshape(Bsz, T, M_HEADS, M_DK)
    k = z[..., OFF_K:OFF_V].reshape(Bsz, T, M_HEADS, M_DK) * (M_DK ** -0.5)
    v = z[..., OFF_V:OFF_O].reshape(Bsz, T, M_HEADS, M_DV)
    o_pre = z[..., OFF_O:OFF_I]
    logi = z[..., OFF_I:OFF_F].astype(jnp.float32)
    logf = jax.nn.log_sigmoid(z[..., OFF_F:OFF_B].astype(jnp.float32))
    hm, (C1, n1, m1) = _mlstm(q, k, v, logi, logf, C0, n0, m0)
    hm = _group_rms(hm.astype(x.dtype).reshape(Bsz, T, M_WIDTH), M_HEADS) * jax.nn.sigmoid(o_pre)
    bg = z[..., OFF_B:OFF_C]
    u = z[..., OFF_C:OFF_X] * z[..., OFF_X:N_IN]
    ext = jnp.concatenate([conv_prev.astype(u.dtype), u], axis=1)
    yc = conv_w[0] * ext[:, 0:T]
    for j in range(1, CONV_W):
        yc = yc + conv_w[j] * ext[:, j:j + T]
    ys = _group_rms(bg * yc, S_GROUPS)
    mix = jnp.concatenate([hm, ys], axis=-1) * g_mix
    x = x + gt1 * (mix @ w_out)
    h2 = _rms(x) * g2 * (1 + sc2) + sh2
    x = x + gt2 * (jnp.square(jax.nn.relu(h2 @ w_up)) @ w_down)
    return x, (C1, n1, m1, ext[:, -(CONV_W - 1):])


def setup_inputs(seed: int = 0) -> dict:
    key = jax.random.key(seed)
    ks = jax.random.split(key, 20)
    d = D_MODEL

    def nrm(k, shape, s):
        return s * jax.random.normal(k, shape, jnp.float32)

    b_in = nrm(ks[12], (DEPTH, N_IN), 0.02).at[:, OFF_F:OFF_B].add(jnp.linspace(3.0, 6.0, M_HEADS))
    return {
        "x_prompt": nrm(ks[0], (BATCH, SEQ, d), 1.0),
        "x_sample": nrm(ks[1], (DEC_BATCH, DEC_SEQ, d), 1.0),
        "c_prompt": nrm(ks[2], (BATCH, d), 1.0),
        "c_sample": nrm(ks[3], (DEC_BATCH, d), 1.0),
        "state_C": nrm(ks[4], (DEPTH, DEC_BATCH, M_HEADS, M_DK, M_DV), 0.1),
        "state_n": nrm(ks[5], (DEPTH, DEC_BATCH, M_HEADS, M_DK), 0.1),
        "state_m": nrm(ks[6], (DEPTH, DEC_BATCH, M_HEADS), 0.5),
        "state_conv": nrm(ks[7], (DEPTH, DEC_BATCH, CONV_W - 1, S_WIDTH), 1.0),
        "w_ada": nrm(ks[8], (DEPTH, d, N_MOD * d), 0.5 * d ** -0.5),
        "b_ada": nrm(ks[9], (DEPTH, N_MOD * d), 0.02),
        "g_norm1": 1.0 + nrm(ks[10], (DEPTH, d), 0.02),
        "w_in": nrm(ks[11], (DEPTH, d, N_IN), d ** -0.5),
        "b_in": b_in,
        "conv_w": nrm(ks[13], (DEPTH, CONV_W, S_WIDTH), CONV_W ** -0.5),
        "g_mix_out": 1.0 + nrm(ks[14], (DEPTH, MIX_WIDTH), 0.02),
        "w_out": nrm(ks[15], (DEPTH, MIX_WIDTH, d), MIX_WIDTH ** -0.5),
        "g_norm2": 1.0 + nrm(ks[16], (DEPTH, d), 0.02),
        "w_up": nrm(ks[17], (DEPTH, d, D_FF), d ** -0.5),
        "w_down": nrm(ks[18], (DEPTH, D_FF, d), D_FF ** -0.5),
        "g_final": 1.0 + nrm(ks[19], (d,), 0.02),
    }


def reference(x_prompt, x_sample, c_prompt, c_sample, state_C, state_n, state_m, state_conv,
              w_ada, b_ada, g_norm1, w_in, b_in, conv_w, g_mix_out, w_out, g_norm2, w_up, w_down, g_final):
    xp, xs = x_prompt, x_sample
    bp = xp.shape[0]
    C0p = jnp.zeros((bp, M_HEADS, M_DK, M_DV), jnp.float32)
    n0p = jnp.zeros((bp, M_HEADS, M_DK), jnp.float32)
    m0p = jnp.zeros((bp, M_HEADS), jnp.float32)
    cv0p = jnp.zeros((bp, CONV_W - 1, S_WIDTH), xp.dtype)
    pC, pn, pm, pcv, sC, sn, sm, scv = [], [], [], [], [], [], [], []
    for l in range(DEPTH):
        wl = (w_ada[l], b_ada[l], g_norm1[l], w_in[l], b_in[l], conv_w[l], g_mix_out[l],
              w_out[l], g_norm2[l], w_up[l], w_down[l])
        xp, (C1, n1, m1, cb) = _layer(xp, c_prompt, C0p, n0p, m0p, cv0p, *wl)
        pC.append(C1); pn.append(n1); pm.append(m1); pcv.append(cb)
        xs, (C2, n2, m2, cb2) = _layer(xs, c_sample, state_C[l], state_n[l], state_m[l], state_conv[l], *wl)
        sC.append(C2); sn.append(n2); sm.append(m2); scv.append(cb2)
    y_prompt = _rms(xp) * g_final
    y_sample = _rms(xs) * g_final
    return (y_prompt, y_sample,
            jnp.stack(pC), jnp.stack(pn), jnp.stack(pm), jnp.stack(pcv),
            jnp.stack(sC), jnp.stack(sn), jnp.stack(sm), jnp.stack(scv))
```

```python
import numpy as np
import concourse.bass as bass
import concourse.mybir as mybir
from concourse.bass_utils import run_bass_kernel_spmd
from contextlib import ExitStack

F32 = mybir.dt.float32
BF16 = mybir.dt.bfloat16
ALU = mybir.AluOpType
AF = mybir.ActivationFunctionType

NL = 4
D = 1024
T = 2048
NSAMP = 64
NT = T + NSAMP
N_IN = 3592
DFF = 4096
EPS = 1e-6
KSCALE = 128.0 ** -0.5
TILES = [(0, 512), (512, 512), (1024, 512), (1536, 512), (2048, 64)]
NEG = -30000.0
PFW = 112


class Op:
    __slots__ = ("eng", "fn", "deps", "sig", "sigval", "dkey", "dval", "waits", "dneed", "tag")


class Sched:
    def __init__(self):
        self.ops = []
        self.lastw = {}
        self.readers = {}
        self.dma_count = {}
        self.limit = 0
        self.debug = False
        self.dropped = set()

    def add(self, eng, fn, R=(), W=(), dkey=None, ndma=0, force=False):
        op = Op()
        if self.limit and len(self.ops) >= self.limit and not force:
            op.dkey = dkey
            self.dropped.add(id(op))
            self._keep = getattr(self, "_keep", [])
            self._keep.append(op)
            return op
        op.eng = eng
        op.fn = fn
        op.dkey = dkey
        op.sig = False
        op.sigval = 0
        op.dval = 0
        deps = []
        for t in R:
            lw = self.lastw.get(t)
            if lw is not None:
                deps.append(lw)
        for t in W:
            lw = self.lastw.get(t)
            if lw is not None:
                deps.append(lw)
            rd = self.readers.get(t)
            if rd:
                deps.extend(rd.values())
        op.deps = deps
        op.dneed = {}
        if self.debug:
            import sys as _sys
            fr = _sys._getframe(1)
            tg = []
            while fr is not None and len(tg) < 3:
                tg.append(fr.f_lineno)
                fr = fr.f_back
            op.tag = tg
        for d in deps:
            if d.dkey is not None:
                op.dneed[d.dkey] = 16 * self.dma_count[d.dkey]
        for t in W:
            self.lastw[t] = op
            self.readers[t] = {}
        rk = ("d", dkey) if dkey else eng
        for t in R:
            self.readers.setdefault(t, {})[rk] = op
        if dkey:
            self.dma_count[dkey] = self.dma_count.get(dkey, 0) + ndma
            op.dval = 16 * self.dma_count[dkey]
        self.ops.append(op)
        return op

    def finalize(self):
        for op in self.ops:
            for d in op.deps:
                if d.dkey is None and not (d.eng == "pe" and op.eng == "pe"):
                    d.sig = True
        cnt = {}
        waited = {}
        for op in self.ops:
            need = {}
            for d in op.deps:
                if d is op:
                    continue
                if d.dkey is not None:
                    k = ("d", d.dkey)
                    v = op.dneed.get(d.dkey, d.dval)
                else:
                    if d.eng == "pe" and op.eng == "pe":
                        continue
                    k = ("e", d.eng)
                    v = d.sigval
                if v > need.get(k, 0):
                    need[k] = v
            w = waited.setdefault(op.eng, {})
            op.waits = []
            for k, v in need.items():
                if v > w.get(k, 0):
                    w[k] = v
                    op.waits.append((k, v))
            if op.dkey is None and op.sig:
                cnt[op.eng] = cnt.get(op.eng, 0) + 1
                op.sigval = cnt[op.eng]

    def emit(self, nc, es):
        self.finalize()
        sems = {}
        for eng in ("pe", "act", "dve", "pool", "sp"):
            sems[("e", eng)] = es.enter_context(nc.semaphore("c_" + eng))
        for key in self.dma_count:
            sems[("d", key)] = es.enter_context(nc.semaphore("d_" + key))
        block = es.enter_context(nc.Block())
        decos = (("pe", block.tensor), ("act", block.scalar), ("dve", block.vector),
                 ("pool", block.gpsimd), ("sp", block.sync))
        for eng, deco in decos:
            ops_e = [op for op in self.ops if op.eng == eng]

            def body(e, ops_e=ops_e, eng=eng):
                for op in ops_e:
                    for (k, v) in op.waits:
                        e.wait_ge(sems[k], v)
                    if op.dkey is not None:
                        op.fn(e, sems[("d", op.dkey)])
                    else:
                        inst = op.fn(e)
                        if op.sig:
                            inst.then_inc(sems[("e", eng)], 1)
            deco(body)


class Arena:
    def __init__(self, ap, nelem):
        self.ap = ap
        self.n = nelem
        self.off = 0
        self.hi = 0

    def alloc(self, nelem, dt, parts=128):
        n2 = nelem * (2 if dt == F32 else 1)
        n2 = (n2 + 15) // 16 * 16
        a = self.ap[0:parts, self.off:self.off + n2]
        self.off += n2
        self.hi = max(self.hi, self.off)
        assert self.off <= self.n, ("SBUF arena overflow", self.off, self.n)
        if dt == F32:
            a = a.bitcast(F32)
            if a.shape[-1] != nelem:
                a = a[:, 0:nelem]
        else:
            if n2 != nelem:
                a = a[:, 0:nelem]
        return a


def build_nc(nl=NL):
    nc = bass.Bass("TRN2", target_bir_lowering=False)
    S = Sched()
    import os as _os
    S.limit = int(_os.environ.get("K_LIMIT", "0"))
    S.debug = bool(_os.environ.get("K_DEBUG"))
    _NC_CACHE["S"] = S
    es = ExitStack()

    def din(name, shape):
        return nc.dram_tensor(name, shape, F32, kind="ExternalInput").ap()

    def dout(name, shape):
        return nc.dram_tensor(name, shape, F32, kind="ExternalOutput").ap()

    xT_d = din("xT", [128, 8, NT])
    cT_d = din("cT", [128, 40])
    pf_d = din("pf", [128, NL * PFW + 8])
    pg_d = din("pg", [4, 24])
    cst0_d = din("cst0", [NL, 128, 16 * 129])
    sconv0_d = din("sconv0", [NL, 128, 32])
    bkv_d = din("bkv", [NL, 128, 1024])
    w_ada_d = din("w_ada", [NL, D, 6 * D])
    w_in_d = din("w_in", [NL, D, N_IN])
    w_out_d = din("w_out", [NL, D, D])
    w_up_d = din("w_up", [NL, D, DFF])
    w_down_d = din("w_down", [NL, DFF, D])
    yT_d = dout("yT", [128, 8, NT])
    cst_d = dout("cst", [NL, 128, 20 * 129])
    mout_d = dout("mout", [4, NL * 5])
    convout_d = dout("convout", [NL, 128, 40])

    ARENA_ELEMS = 212800 // 2
    arena_t = es.enter_context(nc.sbuf_tensor("arena", [128, ARENA_ELEMS], BF16))
    AR = Arena(arena_t, ARENA_ELEMS)
    PS = [es.enter_context(nc.psum_tensor("psb%d" % i, [128, 512], F32)) for i in range(8)]

    def pst(i):
        return ("ps", i)

    xT = AR.alloc(8 * NT, F32).rearrange("p (k t) -> p k t", k=8)
    pf = AR.alloc(NL * PFW + 8, F32)
    pg = AR.alloc(24, F32, parts=4)
    cTs = AR.alloc(40, F32).rearrange("p (k b) -> p k b", k=8)
    scb = AR.alloc(40, BF16).rearrange("p (k b) -> p k b", k=8)
    ones_bf = AR.alloc(128, BF16)
    ones4w = AR.alloc(512, F32, parts=4)
    SEL = AR.alloc(512, F32, parts=4).rearrange("p (h m) -> p h m", h=4)
    rAab = AR.alloc(512, BF16, parts=4)
    pgn = AR.alloc(16, F32, parts=4)
    Mcarn = AR.alloc(1, F32, parts=4)
    modS = [AR.alloc(240, F32).rearrange("p (c b) -> p c b", c=48) for _ in range(2)]
    A1 = [AR.alloc(40, F32).rearrange("p (c b) -> p c b", c=8) for _ in range(2)]
    A2 = [AR.alloc(40, F32).rearrange("p (c b) -> p c b", c=8) for _ in range(2)]
    bks = AR.alloc(NL * 4, F32)
    negbf = AR.alloc(NL, F32, parts=4)
    Bcar = AR.alloc(1, F32, parts=4)
    Mcar = AR.alloc(1, F32, parts=4)
    moutS = AR.alloc(NL * 5, F32, parts=4)
    dec = AR.alloc(32, F32).rearrange("p (h c) -> p h c", h=4)
    wada_ring = None
    wout_ring = [AR.alloc(8 * 128, BF16).rearrange("p (k n) -> p k n", k=8) for _ in range(3)]
    pp = [AR.alloc(512, F32) for _ in range(4)]
    sbs = [AR.alloc(512, BF16) for _ in range(4)]
    qgs = [AR.alloc(512, BF16) for _ in range(4)]
    rows = [AR.alloc(512, F32, parts=4) for _ in range(5)]
    rGb = AR.alloc(512, BF16, parts=4)
    rEb = AR.alloc(512, BF16, parts=4)
    SELb = AR.alloc(512, BF16, parts=4).rearrange("p (h m) -> p h m", h=4)
    maskPb = AR.alloc(256, BF16).rearrange("p (h m) -> p h m", h=4)
    maskSb = AR.alloc(256, BF16).rearrange("p (h m) -> p h m", h=4)
    mark = AR.off

    hT_flat = AR.alloc(8 * 512, BF16)
    hT = hT_flat.rearrange("p (k t) -> p k t", k=8)
    pq = [hT_flat[:, 1024 * i:1024 * (i + 1)].bitcast(F32) for i in range(4)]
    dummy = AR.alloc(16, F32)
    mixT = AR.alloc(8 * 512, BF16).rearrange("p (k t) -> p k t", k=8)
    AR_qk_flat = AR.alloc(8 * 512, BF16)
    AR_qk = AR_qk_flat.rearrange("p (k t) -> p k t", k=8)
    qT = AR_qk_flat[:, 0:2048].rearrange("p (k t) -> p k t", k=4)
    kT = AR_qk_flat[:, 2048:4096].rearrange("p (k t) -> p k t", k=4)
    sigo = AR.alloc(4 * 512, BF16).rearrange("p (k t) -> p k t", k=4)
    k_tok = AR.alloc(4 * 512, BF16).rearrange("p (s h d) -> p s h d", s=4, h=4)
    v_tok = AR.alloc(4 * 4 * 130, BF16).rearrange("p (s h d) -> p s h d", s=4, h=4)
    swT4 = AR.alloc(4 * 256, BF16).rearrange("p (s h t) -> p s h t", s=4, h=4)
    kw4 = AR.alloc(4 * 512, BF16).rearrange("p (s h d) -> p s h d", s=4, h=4)
    uP = AR.alloc(4 * 514, BF16).rearrange("p (j t) -> p j t", j=4)
    uS = AR.alloc(4 * 4 * 18, BF16).rearrange("p (j s t) -> p j s t", j=4, s=4)
    CextP = AR.alloc(4 * 129, F32).rearrange("p (h e) -> p h e", h=4)
    CextS = AR.alloc(16 * 129, F32).rearrange("p (g e) -> p g e", g=16)
    Cbf = [AR.alloc(128, BF16) for _ in range(4)]
    n0rep = [AR.alloc(128, BF16) for _ in range(4)]
    bkvS = AR.alloc(1024, BF16)
    WSLOT = 8 * 520
    nring = 4 if (AR.n - AR.off) >= 4 * WSLOT + 64 else (3 if (AR.n - AR.off) >= 3 * WSLOT + 64 else 2)
    win_ring = [AR.alloc(WSLOT, BF16).rearrange("p (k n) -> p k n", k=8) for _ in range(nring)]
    endA = AR.off
    AR.off = mark
    h2T = AR.alloc(8 * NT, BF16).rearrange("p (k t) -> p k t", k=8)
    actT = AR.alloc(4 * NT, BF16).rearrange("p (f t) -> p f t", f=4)
    wu = AR.alloc(8 * 512, BF16).rearrange("p (k n) -> p k n", k=8)
    wd = AR.alloc(4 * 1024, BF16).rearrange("p (f n) -> p f n", f=4)
    sqB = AR.alloc(8 * 512, BF16).rearrange("p (k t) -> p k t", k=8)
    wadaB = [AR.alloc(8 * 128, BF16).rearrange("p (k n) -> p k n", k=8) for _ in range(6)]
    endB = AR.off
    AR.off = max(endA, endB)
    PHASE = ["phaseAB"]

    def act(out, in_, func, bias=0.0, scale=1.0, R=(), W=()):
        return S.add("act", lambda e: e.activation(out=out, in_=in_, func=func, bias=bias, scale=scale), R, W)

    def tt(out, in0, in1, op, R=(), W=(), eng="dve"):
        return S.add(eng, lambda e: e.tensor_tensor(out=out, in0=in0, in1=in1, op=op), R, W)

    def ts(out, in0, s1, s2, op0, op1=None, R=(), W=(), eng="dve"):
        if op1 is None:
            return S.add(eng, lambda e: e.tensor_scalar(out=out, in0=in0, scalar1=s1, scalar2=None, op0=op0), R, W)
        return S.add(eng, lambda e: e.tensor_scalar(out=out, in0=in0, scalar1=s1, scalar2=s2, op0=op0, op1=op1), R, W)

    def stt(out, in0, scalar, in1, op0, op1, R=(), W=()):
        return S.add("dve", lambda e: e.scalar_tensor_tensor(out=out, in0=in0, scalar=scalar, in1=in1, op0=op0, op1=op1), R, W)

    def cpy(out, in_, R=(), W=(), eng="dve"):
        return S.add(eng, lambda e: e.tensor_copy(out=out, in_=in_), R, W)

    def mset(ap, val, W=(), eng="pool"):
        return S.add(eng, lambda e: e.memset(ap, val), (), W)

    def mm(out, lhsT, rhs, start, stop, R=(), W=()):
        return S.add("pe", lambda e: e.matmul(out, lhsT=lhsT, rhs=rhs, start=start, stop=stop), R, W)

    def scan(out, d0, d1, init, op0, op1, R=(), W=()):
        return S.add("dve", lambda e: e.tensor_tensor_scan(out=out, data0=d0, data1=d1, initial=init, op0=op0, op1=op1), R, W)

    def dma(eng, key, pairs, R=(), W=(), cast=False):
        def fn(e, sem):
            for (o, i) in pairs:
                e.dma_start(out=o, in_=i).then_inc(sem, 16)
        return S.add(eng, fn, R, W, dkey=key, ndma=len(pairs))

    bigc = [0]
    bigpool = [[0, 1]]

    def bigbank():
        pool_ = bigpool[0]
        b = pool_[bigc[0] % len(pool_)]
        bigc[0] += 1
        return b

    out_ops = []

    for tt_i, (c0, n) in enumerate(TILES):
        dma("sp", "xin%d" % tt_i, [(xT[:, :, c0:c0 + n], xT_d[:, :, c0:c0 + n])], W=[("x", tt_i)])
    dma("sp", "pf", [(pf, pf_d[:, :])], W=["pf"])
    dma("sp", "pg", [(pg, pg_d[:, :])], W=["pg"])
    dma("sp", "cT", [(cTs, cT_d[:, :].rearrange("p (k b) -> p k b", k=8))], W=["cT"])
    mset(ones_bf, 1.0, W=["ones_bf"])
    mset(ones4w, 1.0, W=["ones4w"])
    mset(pp[0], 0.0, W=["pp0"])
    mset(pp[1], 1.0, W=["pp1"])
    ones4v = ones4w[:, 0:128]
    for h in range(4):
        S.add("pool", lambda e, h=h: e.affine_select(out=SEL[:, h, :], in_=ones4v, pattern=[[0, 128]], base=-h,
                                                      channel_multiplier=1, compare_op=ALU.is_equal, fill=0.0),
              R=["ones4w"], W=["SEL"])
    ts(pgn, pg[:, 8:24], -1.0, None, ALU.mult, R=["pg"], W=["pgn"])
    for hf in range(2):
        o64 = pp[1][64 * hf:64 * hf + 64, 0:256].rearrange("p (h m) -> p h m", h=4)
        mP = maskPb[64 * hf:64 * hf + 64]
        mS = maskSb[64 * hf:64 * hf + 64]
        S.add("pool", lambda e, o64=o64, mP=mP: e.affine_select(out=mP, in_=o64, pattern=[[0, 4], [1, 64]], base=0,
                                                                  channel_multiplier=-1, compare_op=ALU.is_ge, fill=0.0),
              R=["pp1"], W=["maskPb"])
        S.add("pool", lambda e, o64=o64, mS=mS: e.affine_select(out=mS, in_=o64, pattern=[[0, 4], [1, 64]], base=0,
                                                                  channel_multiplier=-1, compare_op=ALU.is_ge, fill=0.0),
              R=["pp1"], W=["maskSb"])
        for j in range(1, 4):
            S.add("pool", lambda e, j=j, mS=mS: e.affine_select(out=mS[:, :, 16 * j:16 * j + 16], in_=mS[:, :, 16 * j:16 * j + 16],
                                                                  pattern=[[0, 4], [0, 16]], base=-16 * j, channel_multiplier=1,
                                                                  compare_op=ALU.is_ge, fill=0.0),
                  R=["maskSb"], W=["maskSb"])
    act(scb, cTs, AF.Silu, R=["cT"], W=["scb"])
    cpy(SELb, SEL, R=["SEL"], W=["SELb"])
    for l in range(NL):
        ts(bks[:, 4 * l:4 * l + 4], pf[:, PFW * l + 72 + 4:PFW * l + 72 + 8], KSCALE, None, ALU.mult, R=["pf"], W=["bks"])
        ts(negbf[:, l:l + 1], pg[:, 2 * l + 1:2 * l + 2], -1.0, None, ALU.mult, R=["pg"], W=["negbf"])

    def pfc(l, off, k=None):
        base = PFW * l + off
        if k is None:
            return pf[:, base:base + 1]
        return pf[:, base + k:base + k + 1]

    MODBANK = 7
    wada_cnt = [0]

    def emit_mod_blocks(l, cb0, cb1, use_win=False):
        wv = w_ada_d[l].rearrange("(k p) n -> p k n", p=128)
        if use_win:
            for g in range(cb0 // 4, cb1 // 4):
                slot = win_cnt[0] % nring
                win_cnt[0] += 1
                dma("pool", "win%d" % slot, [(win_ring[slot][:, :, 0:512], wv[:, :, g * 512:(g + 1) * 512])], W=[("win", slot)])
                for c4 in range(4):
                    cb = 4 * g + c4
                    for kc in range(8):
                        mm(PS[MODBANK][:, cb * 5:cb * 5 + 5], win_ring[slot][:, kc, c4 * 128:(c4 + 1) * 128], scb[:, kc, :],
                           kc == 0, kc == 7, R=[("win", slot), "scb"], W=[pst(MODBANK)])
        for cb in (range(cb0, cb1) if not use_win else ()):
            slot = wada_cnt[0] % 2
            wada_cnt[0] += 1
            dma("pool", "wada%d" % slot, [(wada_ring[slot], wv[:, :, cb * 128:(cb + 1) * 128])], W=[("wada", slot)])
            for kc in range(8):
                mm(PS[MODBANK][:, cb * 5:cb * 5 + 5], wada_ring[slot][:, kc, :], scb[:, kc, :], kc == 0, kc == 7,
                   R=[("wada", slot), "scb"], W=[pst(MODBANK)])
        if cb1 == 48:
            p = l % 2
            ms = modS[p]
            tt(ms, PS[MODBANK][:, 0:240].rearrange("p (c b) -> p c b", c=48),
               pf[:, PFW * l:PFW * l + 48].rearrange("p (c o) -> p c o", o=1).to_broadcast([128, 48, 5]), ALU.add,
               R=[pst(MODBANK), "pf"], W=[("mod", p)])
            g1b = pf[:, PFW * l + 48:PFW * l + 56].rearrange("p (c o) -> p c o", o=1).to_broadcast([128, 8, 5])
            g2b = pf[:, PFW * l + 56:PFW * l + 64].rearrange("p (c o) -> p c o", o=1).to_broadcast([128, 8, 5])
            stt(A1[p], ms[:, 8:16, :], 1.0, g1b, ALU.add, ALU.mult, R=[("mod", p), "pf"], W=[("modA", p)])
            stt(A2[p], ms[:, 32:40, :], 1.0, g2b, ALU.add, ALU.mult, R=[("mod", p), "pf"], W=[("modA", p)])

    def modB_load(l, q):
        wv = w_ada_d[l].rearrange("(k p) n -> p k n", p=128)
        for i in range(6):
            cb = 6 * q + i
            dma("pool", "wadaB%d" % i, [(wadaB[i], wv[:, :, cb * 128:(cb + 1) * 128])], R=PHASE, W=[("wadaB", i)])

    def modB_compute(l, q):
        for i in range(6):
            cb = 6 * q + i
            for kc in range(8):
                mm(PS[MODBANK][:, cb * 5:cb * 5 + 5], wadaB[i][:, kc, :], scb[:, kc, :], kc == 0, kc == 7,
                   R=[("wadaB", i), "scb"], W=[pst(MODBANK)])
        if q == 7:
            emit_mod_blocks(l, 48, 48)

    def rms_rstd(tt_i, c0, n, sq8, sq_tok, dst, dst_tok, extra=()):
        act(sq8[:, :, 0:n], xT[:, :, c0:c0 + n], AF.Square, R=[("x", tt_i)] + PHASE, W=[sq_tok] + list(extra))
        b = bigbank()
        for kc in range(8):
            mm(PS[b][:, 0:n], ones_bf, sq8[:, kc, 0:n], kc == 0, kc == 7, R=[sq_tok, "ones_bf"] + list(extra), W=[pst(b)])
        act(dst[:, 0:n], PS[b][:, 0:n], AF.Ln, bias=EPS, scale=1.0 / D, R=[pst(b)], W=[dst_tok])
        act(dst[:, 0:n], dst[:, 0:n], AF.Exp, scale=-0.5, R=[dst_tok], W=[dst_tok])

    def segs_of(tt_i, n):
        if tt_i < 4:
            return [(0, n, 0)]
        return [(16 * j, 16, 1 + j) for j in range(4)]

    win_cnt = [0]
    wout_cnt = [0]

    def phase_A(l):
        p = l % 2
        mtok = [("mod", p), ("modA", p)]
        wv = w_in_d[l].rearrange("(k p) n -> p k n", p=128)
        wov = w_out_d[l].rearrange("(k p) n -> p k n", p=128)
        dma("pool", "bkv", [(bkvS, bkv_d[l])], R=PHASE, W=["bkv"])
        mset(CextP, 0.0, W=[("cx", h) for h in range(4)], eng="pool")
        dma("sp", "cst0", [(CextS, cst0_d[l].rearrange("p (g e) -> p g e", g=16))], W=[("cx", 4 + g) for g in range(16)])
        mset(uP[:, :, 0:2], 0.0, W=["uP"], eng="pool")
        dma("pool", "sconv0", [(uS[:, :, :, 0:2], sconv0_d[l].rearrange("p (j s r) -> p j s r", j=4, s=4))], W=["uS"])
        mset(v_tok[:, :, :, 128:129], 1.0, W=["v_tok"], eng="pool")

        blocks = [("OG", [(1536, 520)]), ("V", [(1024, 512)])]
        for j in range(4):
            blocks.append(("C%d" % j, [(2056 + 128 * j, 128), (2568 + 128 * j, 128), (3080 + 128 * j, 128)]))
        blocks += [("Q", [(0, 512)]), ("K", [(512, 512)])]

        nb = len(blocks)
        total_blocks = nb * len(TILES)
        issued = [0]
        slot_of = {}

        def ensure_issued(g):
            while issued[0] <= g and issued[0] < total_blocks:
                gi = issued[0]
                bi_ = gi % nb
                slot = win_cnt[0] % nring
                win_cnt[0] += 1
                pairs = []
                o = 0
                for (cs, cn) in blocks[bi_][1]:
                    pairs.append((win_ring[slot][:, :, o:o + cn], wv[:, :, cs:cs + cn]))
                    o += cn
                dma("pool", "win%d" % slot, pairs, W=[("win", slot)])
                slot_of[gi] = slot
                issued[0] += 1

        conv_tail = [None]
        sq8A = AR_qk

        def emit_norm1(ti):
            c0_, n_ = TILES[ti]
            segs_ = segs_of(ti, n_)
            xt_ = ("x", ti)
            rms_rstd(ti, c0_, n_, sq8A, "qkT", pp[0], "pp0", extra=["qT", "kT"])
            for kc in range(8):
                tmp = pp[1 + kc % 2]
                tk = "pp%d" % (1 + kc % 2)
                tt(tmp[:, 0:n_], xT[:, kc, c0_:c0_ + n_], pp[0][:, 0:n_], ALU.mult, R=[xt_, "pp0"], W=[tk])
                for (so, sn, b) in segs_:
                    act(hT[:, kc, so:so + sn], tmp[:, so:so + sn], AF.Identity, bias=modS[p][:, kc, b:b + 1],
                        scale=A1[p][:, kc, b:b + 1], R=[tk] + mtok, W=["hT"])

        for tt_i, (c0, n) in enumerate(TILES):
            segs = segs_of(tt_i, n)
            nsub = max(1, n // 128)
            subP = 128 if n >= 128 else n
            nch = max(1, n // 64)
            xt = ("x", tt_i)
            bigpool[0] = [0, 1, 2, 5, 6, 7, 3, 4]
            if tt_i == 0:
                emit_norm1(0)
            rA, rL, rB, rM, rG = rows
            for bi in range(nb):
                G = tt_i * nb + bi
                ensure_issued(G + nring - 1)
                slot = slot_of[G]
                wsl = win_ring[slot]
                wtk = ("win", slot)
                name = blocks[bi][0]

                def fm_group(col, M, evac):
                    b = bigbank()
                    for kc in range(8):
                        mm(PS[b][0:M, 0:n], wsl[:, kc, col:col + M], hT[:, kc, 0:n], kc == 0, kc == 7,
                           R=[wtk, "hT"], W=[pst(b)])
                    evac(PS[b][0:M, 0:n], pst(b))

                if name == "OG":
                    for j in range(4):
                        fm_group(128 * j, 128, lambda ps, ptk, j=j: act(sigo[:, j, 0:n], ps, AF.Sigmoid, bias=pfc(l, 72, 12 + j),
                                                                        R=[ptk, "pf"], W=["sigo"]))
                    fm_group(512, 4, lambda ps, ptk: act(rA[:, 0:n], ps, AF.Identity, bias=pg[:, 2 * l:2 * l + 1],
                                                         R=[ptk, "pg"], W=["rA"]))

                    def evF(ps, ptk):
                        act(rL[:, 0:n], ps, AF.Exp, bias=negbf[:, l:l + 1], scale=-1.0, R=[ptk, "negbf"], W=["rL"])
                        act(rL[:, 0:n], rL[:, 0:n], AF.Ln, bias=1.0, scale=1.0, R=["rL"], W=["rL"])
                    fm_group(516, 4, evF)
                    for si, (so, sn, b) in enumerate(segs):
                        if tt_i == 4:
                            initB = 0.0
                            initM = pg[:, 8 + 4 * l + si:8 + 4 * l + si + 1]
                        elif tt_i == 0:
                            initB = 0.0
                            initM = 0.0
                        else:
                            initB = Bcar[:, 0:1]
                            initM = Mcar[:, 0:1]
                        scan(rB[:, so:so + sn], ones4w[:, 0:sn], rL[:, so:so + sn], initB, ALU.mult, ALU.add,
                             R=["rL", "ones4w", "Bcar"], W=["rB"])
                        tt(rA[:, so:so + sn], rA[:, so:so + sn], rB[:, so:so + sn], ALU.add, R=["rA", "rB"], W=["rA"])
                        scan(rM[:, so:so + sn], ones4w[:, 0:sn], rA[:, so:so + sn], initM, ALU.mult, ALU.max,
                             R=["rA", "ones4w", "Mcar", "pg"], W=["rM"])
                        if tt_i == 4:
                            act(rG[:, so:so + sn], rM[:, so:so + sn], AF.Exp, bias=initM, scale=-1.0, R=["rM", "pg"], W=["rG"])
                        else:
                            for ci in range(nch):
                                cc = 64 * ci
                                if ci == 0:
                                    bias = initM
                                else:
                                    bias = rM[:, cc - 1:cc]
                                act(rG[:, cc:cc + 64], rM[:, cc:cc + 64], AF.Exp, bias=bias, scale=-1.0,
                                    R=["rM", "Mcar"], W=["rG"])
                        if tt_i == 3 or tt_i == 4:
                            e_ = so + sn
                            tt(moutS[:, 5 * l + b:5 * l + b + 1], rM[:, e_ - 1:e_], rB[:, e_ - 1:e_], ALU.subtract,
                               R=["rM", "rB"], W=["moutS"])
                    if tt_i < 3:
                        cpy(Bcar[:, 0:1], rB[:, n - 1:n], R=["rB"], W=["Bcar"])
                    tt(rL[:, 0:n], rB[:, 0:n], rM[:, 0:n], ALU.subtract, R=["rB", "rM"], W=["rL"])
                    act(rEb[:, 0:n], rL[:, 0:n], AF.Exp, R=["rL"], W=["rEb"])
                    cpy(rGb[:, 0:n], rG[:, 0:n], R=["rG"], W=["rGb"])
                    ts(rL[:, 0:n], rM[:, 0:n], -1.0, None, ALU.mult, R=["rM", "rEb"], W=["rL"])
                    if tt_i == 4:
                        for si, (so, sn, b) in enumerate(segs):
                            act(rAab[:, so:so + sn], rA[:, so:so + sn], AF.Exp, bias=pgn[:, 4 * l + si:4 * l + si + 1],
                                R=["rA", "pgn"], W=["rAab"])
                            act(rB[:, so:so + sn], rA[:, so:so + sn], AF.Exp, bias=rL[:, so + sn - 1:so + sn],
                                R=["rA", "rL", "moutS", "Bcar"], W=["rB"])
                    else:
                        for ci in range(nch):
                            cc = 64 * ci
                            if ci == 0:
                                bias = 0.0 if tt_i == 0 else Mcarn[:, 0:1]
                            else:
                                bias = rL[:, cc - 1:cc]
                            act(rAab[:, cc:cc + 64], rA[:, cc:cc + 64], AF.Exp, bias=bias, R=["rA", "rL", "Mcarn"], W=["rAab"])
                            act(rB[:, cc:cc + 64], rA[:, cc:cc + 64], AF.Exp, bias=rL[:, cc + 63:cc + 64],
                                R=["rA", "rL", "moutS", "Bcar"], W=["rB"])
                    if tt_i < 3:
                        cpy(Mcar[:, 0:1], rM[:, n - 1:n], R=["rM"], W=["Mcar"])
                        ts(Mcarn[:, 0:1], rM[:, n - 1:n], -1.0, None, ALU.mult, R=["rM", "rAab"], W=["Mcarn"])
                elif name == "Q":
                    for j in range(4):
                        gbS = sbs[2 + j % 2]
                        gbk = "sb%d" % (2 + j % 2)
                        b_ = bigbank()
                        mm(PS[b_][:, 0:n], SELb[:, j, :], rGb[:, 0:n], True, True, R=["rGb", "SELb"], W=[pst(b_)])
                        act(gbS[:, 0:n], PS[b_][:, 0:n], AF.Copy, R=[pst(b_)], W=[gbk])
                        fm_group(128 * j, 128, lambda ps, ptk, j=j, gbS=gbS, gbk=gbk: stt(
                            qgs[j][:, 0:n], ps, pfc(l, 72, j), gbS[:, 0:n], ALU.add, ALU.mult,
                            R=[ptk, "pf", gbk], W=[("qg", j)]))
                        if j == 0 and conv_tail[0] is not None:
                            conv_tail[0]()
                            conv_tail[0] = None
                        bd = bigbank()
                        if tt_i < 4:
                            mm(PS[bd][:, 0:nch], SEL[:, j, :], rG[:, 63:n:64], True, True, R=["rG", "SEL"], W=[pst(bd)])
                            cpy(dec[:, j, 0:nch], PS[bd][:, 0:nch], R=[pst(bd)], W=[("dec", j)])
                        else:
                            mm(PS[bd][:, 0:4], SEL[:, j, :], rG[:, 15:64:16], True, True, R=["rG", "SEL"], W=[pst(bd)])
                            cpy(dec[:, j, 0:4], PS[bd][:, 0:4], R=[pst(bd)], W=[("dec", j)])
                elif name in ("K", "V"):
                    if name == "K":
                        for j in range(4):
                            abS = sbs[2 + j % 2]
                            abk = "sb%d" % (2 + j % 2)
                            b_ = bigbank()
                            mm(PS[b_][:, 0:n], SELb[:, j, :], rAab[:, 0:n], True, True, R=["rAab", "SELb"], W=[pst(b_)])
                            act(abS[:, 0:n], PS[b_][:, 0:n], AF.Copy, scale=KSCALE, R=[pst(b_)], W=[abk])
                            fm_group(128 * j, 128, lambda ps, ptk, j=j, abS=abS, abk=abk: stt(
                                kT[:, j, 0:n], ps, pfc(l, 72, 4 + j), abS[:, 0:n], ALU.add, ALU.mult,
                                R=[ptk, "pf", abk], W=["kT"]))
                    for st in range(nsub):
                        b = bigbank()
                        for kc in range(8):
                            mm(PS[b][0:subP, :], hT[:, kc, st * 128:st * 128 + subP], wsl[:, kc, 0:512], kc == 0, kc == 7,
                               R=[wtk, "hT"], W=[pst(b)])
                        psv = PS[b][0:subP, :].rearrange("p (h d) -> p h d", h=4)
                        if name == "K":
                            tt(k_tok[0:subP, st, :, :], psv, bkvS[0:subP, 0:512].rearrange("p (h d) -> p h d", h=4),
                               ALU.add, R=[pst(b), "bkv"], W=["k_tok"])
                        else:
                            tt(v_tok[0:subP, st, :, 0:128], psv, bkvS[0:subP, 512:1024].rearrange("p (h d) -> p h d", h=4),
                               ALU.add, R=[pst(b), "bkv"], W=["v_tok"])
                else:
                    j = int(name[1])
                    cTj, bgj = sbs[0], sbs[1]
                    sqj = sbs[2 + j % 2]
                    sqk = "sb%d" % (2 + j % 2)
                    if conv_tail[0] is not None:
                        prev_tail = conv_tail[0]
                        conv_tail[0] = None
                    else:
                        prev_tail = None
                    fm_group(0, 128, lambda ps, ptk: act(bgj[:, 0:n], ps, AF.Identity, bias=pfc(l, 72, 16 + j),
                                                         R=[ptk, "pf"], W=["sb1"]))
                    fm_group(128, 128, lambda ps, ptk: act(cTj[:, 0:n], ps, AF.Identity, bias=pfc(l, 72, 20 + j),
                                                           R=[ptk, "pf"], W=["sb0"]))
                    if tt_i < 4:
                        uw = uP[:, j, 2:2 + n]
                        utk = "uP"
                        ctv = cTj[:, 0:n]
                    else:
                        uw = uS[:, j, :, 2:18]
                        utk = "uS"
                        ctv = cTj[:, 0:n].rearrange("p (s t) -> p s t", s=4)

                    def evX(ps, ptk):
                        pv = ps if tt_i < 4 else ps.rearrange("p (s t) -> p s t", s=4)
                        stt(uw, pv, pfc(l, 72, 24 + j), ctv, ALU.add, ALU.mult, R=[ptk, "pf", "sb0"], W=[utk])
                    fm_group(256, 128, evX)
                    if prev_tail is not None:
                        prev_tail()
                    acc, ybg, rr = pp[1], pp[2 + j % 2], pp[0]
                    ybk = "pp%d" % (2 + j % 2)
                    if tt_i < 4:
                        taps = [uP[:, j, k:k + n] for k in range(3)]
                        accv, ybgv, bgv = acc[:, 0:n], ybg[:, 0:n], bgj[:, 0:n]
                    else:
                        taps = [uS[:, j, :, k:k + 16] for k in range(3)]
                        accv = acc[:, 0:n].rearrange("p (s t) -> p s t", s=4)
                        ybgv = ybg[:, 0:n].rearrange("p (s t) -> p s t", s=4)
                        bgv = bgj[:, 0:n].rearrange("p (s t) -> p s t", s=4)
                    ts(accv, taps[0], pfc(l, 100, 3 * j + 0), None, ALU.mult, R=[utk, "pf"], W=["pp1"])
                    stt(accv, taps[1], pfc(l, 100, 3 * j + 1), accv, ALU.mult, ALU.add, R=[utk, "pf", "pp1"], W=["pp1"])
                    stt(accv, taps[2], pfc(l, 100, 3 * j + 2), accv, ALU.mult, ALU.add, R=[utk, "pf", "pp1"], W=["pp1"])
                    tt(ybgv, accv, bgv, ALU.mult, R=["pp1", "sb1"], W=[ybk])
                    act(sqj[:, 0:n], ybg[:, 0:n], AF.Square, R=[ybk], W=[sqk])

                    def tail(j=j, sqj=sqj, sqk=sqk, ybg=ybg, ybk=ybk, rr=rr):
                        b = bigbank()
                        mm(PS[b][:, 0:n], ones_bf, sqj[:, 0:n], True, True, R=[sqk, "ones_bf"], W=[pst(b)])
                        act(rr[:, 0:n], PS[b][:, 0:n], AF.Ln, bias=EPS, scale=1.0 / 128, R=[pst(b)], W=["pp0"])
                        act(rr[:, 0:n], rr[:, 0:n], AF.Exp, scale=-0.5, R=["pp0"], W=["pp0"])
                        stt(mixT[:, 4 + j, 0:n], ybg[:, 0:n], pfc(l, 64, 4 + j), rr[:, 0:n], ALU.mult, ALU.mult,
                            R=[ybk, "pp0", "pf"], W=["mixT"])
                    conv_tail[0] = tail
                    if j == 3:
                        if tt_i < 3:
                            cpy(uP[:, :, 0:2], uP[:, :, 512:514], R=["uP"], W=["uP"])
                        elif tt_i == 3:
                            out_ops.append(dma("pool", "oconv", [(convout_d[l].rearrange("p (j s r) -> p j s r", j=4, s=5)[:, :, 0, :],
                                                                   uP[:, :, 512:514])], R=["uP"]))
                        else:
                            out_ops.append(dma("pool", "oconv", [(convout_d[l].rearrange("p (j s r) -> p j s r", j=4, s=5)[:, :, 1:5, :],
                                                                   uS[:, :, :, 16:18])], R=["uS"]))

            wout_slot = {}

            def wout_load(oc):
                slot = wout_cnt[0] % 3
                wout_cnt[0] += 1
                dma("pool", "wout%d" % slot, [(wout_ring[slot], wov[:, :, oc * 128:(oc + 1) * 128])], W=[("wout", slot)])
                wout_slot[oc] = slot
            for oc in range(3):
                wout_load(oc)
            if conv_tail[0] is not None:
                conv_tail[0]()
                conv_tail[0] = None
            bigpool[0] = [0, 1]
            mask = maskPb if tt_i < 4 else maskSb
            for ci in range(nch):
                cc = 64 * ci
                half = ci % 2
                st = ci // 2
                P0 = 64 * half
                lb = 3 + (ci % 2)
                ST = PS[lb][P0:P0 + 64, 0:256]
                WC = PS[lb][P0:P0 + 64, 256:260]
                for h in range(4):
                    mm(ST[:, 64 * h:64 * h + 64], kT[:, h, cc:cc + 64], qgs[h][:, cc:cc + 64], True, True,
                       R=["kT", ("qg", h)], W=[pst(lb)])
                mm(WC, rB[:, cc:cc + 64], SEL[:, :, 0:1], True, True, R=["rB", "SEL"], W=[pst(lb)])
                tt(swT4[P0:P0 + 64, st, :, :], ST.rearrange("p (h t) -> p h t", h=4), mask[P0:P0 + 64], ALU.mult,
                   R=[pst(lb), "maskPb", "maskSb"], W=["swT4"])
                wcb = WC.rearrange("p (h o) -> p h o", o=1)
                if tt_i < 4:
                    stt(kw4[P0:P0 + 64, st, :, :], k_tok[P0:P0 + 64, st, :, :], KSCALE,
                        wcb.to_broadcast([64, 4, 128]), ALU.mult, ALU.mult, R=["k_tok", pst(lb)], W=["kw4"])
                else:
                    for g in range(4):
                        wg = sbs[0][0:64, 4 * g:4 * g + 4]
                        tt(wg, WC, mask[0:64, :, 16 * g + 15], ALU.mult, R=[pst(lb), "maskSb"], W=["sb0"])
                        stt(kw4[0:64, g, :, :], k_tok[0:64, 0, :, :], KSCALE,
                            wg.rearrange("p (h o) -> p h o", o=1).to_broadcast([64, 4, 128]), ALU.mult, ALU.mult,
                            R=["k_tok", "sb0"], W=["kw4"])

            NUMBs, DENBs, CUBs = [5, 3], [6, 4], [2, 7]

            units = []
            for ci in range(nch):
                cc = 64 * ci
                st = ci // 2
                P0 = 64 * (ci % 2)
                if tt_i < 4:
                    groups = [(cc, 64, None, st, ci)]
                else:
                    groups = [(16 * g, 16, g, g, g) for g in range(4)]
                for gi, (gc, gn, sg, kslot, dci) in enumerate(groups):
                    units.append(dict(ci=ci, cc=cc, st=st, P0=P0, gc=gc, gn=gn, sg=sg, kslot=kslot, dci=dci,
                                      first=(gi == 0), last=(gi == len(groups) - 1)))

            def emit_cupd(ui, h):
                u = units[ui]
                P0, st = u["P0"], u["st"]
                cb_ = CUBs[h % 2]
                mm(PS[cb_][:, 0:129], kw4[P0:P0 + 64, u["kslot"], h, :], v_tok[P0:P0 + 64, st, h, 0:129], True, True,
                   R=["kw4", "v_tok"], W=[pst(cb_)])

            def emit_unit(ui, h):
                u = units[ui]
                P0, st, cc = u["P0"], u["st"], u["cc"]
                sidx = h if u["sg"] is None else 4 + 4 * u["sg"] + h
                cx = CextP[:, sidx, :] if sidx < 4 else CextS[:, sidx - 4, :]
                ctk = ("cx", sidx)
                cb_ = CUBs[h % 2]
                NUMB, DENB = NUMBs[h % 2], DENBs[h % 2]
                sc = 2 * (h % 2) + ui % 2
                qg = qgs[h]
                qtk = ("qg", h)
                act(Cbf[sc], cx[:, 0:128], AF.Copy, R=[ctk], W=[("Cbf", sc)])
                ts(n0rep[sc], ones_bf, cx[:, 128:129], 1.0, ALU.mult, ALU.mult, R=[ctk, "ones_bf"], W=[("n0rep", sc)],
                   eng="pool")
                stt(cx, cx, dec[:, h, u["dci"]:u["dci"] + 1], PS[cb_][:, 0:129], ALU.mult, ALU.add,
                    R=[ctk, ("dec", h), pst(cb_)], W=[ctk])
                if tt_i == 4:
                    out_ops.append(dma("sp", "ocst", [(cst_d[l].rearrange("p (g e) -> p g e", g=20)[:, sidx, :], cx)], R=[ctk]))
                elif tt_i == 3 and ui == len(units) - 1:
                    out_ops.append(dma("sp", "ocst", [(cst_d[l].rearrange("p (g e) -> p g e", g=20)[:, h, :], cx)], R=[ctk]))
                if u["first"]:
                    sw = swT4[P0:P0 + 64, st, h, :]
                    mm(PS[NUMB][:, cc:cc + 64], v_tok[P0:P0 + 64, st, h, 0:128], sw, True, False,
                       R=["v_tok", "swT4"], W=[pst(NUMB)])
                    mm(PS[DENB][:, cc:cc + 64], ones_bf[P0:P0 + 64, :], sw, True, False,
                       R=["ones_bf", "swT4"], W=[pst(DENB)])
                if ui + 1 < len(units):
                    emit_cupd(ui + 1, h)
                gc, gn = u["gc"], u["gn"]
                mm(PS[NUMB][:, gc:gc + gn], Cbf[sc], qg[:, gc:gc + gn], False, u["last"],
                   R=[("Cbf", sc), qtk], W=[pst(NUMB)])
                mm(PS[DENB][:, gc:gc + gn], n0rep[sc], qg[:, gc:gc + gn], False, u["last"],
                   R=[("n0rep", sc), qtk], W=[pst(DENB)])

            def head_post(h, sset):
                NUMB, DENB = NUMBs[h % 2], DENBs[h % 2]
                if sset == 0:
                    numS, Dm, tot, t1 = pp
                    tk = ["pp0", "pp1", "pp2", "pp3"]
                    sq, sqk = sbs[2], "sb2"
                else:
                    numS, Dm, tot, t1 = pq
                    tk = [("pq", i) for i in range(4)]
                    sq, sqk = sbs[3], "sb3"
                act(sq[:, 0:n], PS[NUMB][:, 0:n], AF.Square, R=[pst(NUMB)], W=[sqk])
                yield
                act(Dm[:, 0:n], PS[DENB][:, 0:n], AF.Abs, R=[pst(DENB)], W=[tk[1]])
                b = bigbank()
                mm(PS[b][:, 0:n], SELb[:, h, :], rEb[:, 0:n], True, True, R=["rEb", "SELb"], W=[pst(b)])
                b2 = bigbank()
                mm(PS[b2][:, 0:n], ones_bf, sq[:, 0:n], True, True, R=[sqk, "ones_bf"], W=[pst(b2)])
                yield
                tt(Dm[:, 0:n], Dm[:, 0:n], PS[b][:, 0:n], ALU.max, R=[tk[1], pst(b)], W=[tk[1]])
                yield
                stt(tot[:, 0:n], Dm[:, 0:n], 128.0 * EPS, Dm[:, 0:n], ALU.mult, ALU.mult, R=[tk[1]], W=[tk[2]])
                yield
                tt(tot[:, 0:n], tot[:, 0:n], PS[b2][:, 0:n], ALU.add, R=[tk[2], pst(b2)], W=[tk[2]])
                yield
                act(tot[:, 0:n], tot[:, 0:n], AF.Ln, scale=1.0 / 128, R=[tk[2]], W=[tk[2]])
                yield
                act(tot[:, 0:n], tot[:, 0:n], AF.Exp, scale=-0.5, R=[tk[2]], W=[tk[2]])
                yield
                tt(t1[:, 0:n], PS[NUMB][:, 0:n], tot[:, 0:n], ALU.mult, R=[pst(NUMB), tk[2]], W=[tk[3]])
                yield
                stt(mixT[:, h, 0:n], t1[:, 0:n], pfc(l, 64, h), sigo[:, h, 0:n], ALU.mult, ALU.mult,
                    R=[tk[3], "pf", "sigo"], W=["mixT"])

            pq_tok = [("pq", i) for i in range(4)]
            S.add("dve", lambda e: e.memset(dummy[:, 0:1], 0.0), W=["hT", "dummy"] + pq_tok)
            for hp in (0, 2):
                emit_cupd(0, hp)
                emit_cupd(0, hp + 1)
                for ui in range(len(units)):
                    emit_unit(ui, hp)
                    emit_unit(ui, hp + 1)
                bigpool[0] = [0, 1, 2, 7]
                gens = [head_post(hp, 0), head_post(hp + 1, 1)]
                while gens:
                    for g_ in list(gens):
                        try:
                            next(g_)
                        except StopIteration:
                            gens.remove(g_)
                bigpool[0] = [0, 1]
            bigpool[0] = [0, 1, 2, 5, 6, 7, 3, 4]
            S.add("dve", lambda e: e.memset(dummy[:, 0:1], 0.0), W=["hT", "dummy"] + pq_tok)
            if tt_i + 1 < len(TILES):
                emit_norm1(tt_i + 1)

            for oc in range(8):
                slot = wout_slot[oc]
                b = bigbank()
                for kc in range(8):
                    mm(PS[b][:, 0:n], wout_ring[slot][:, kc, :], mixT[:, kc, 0:n], kc == 0, kc == 7,
                       R=[("wout", slot), "mixT"], W=[pst(b)])
                if oc + 3 < 8:
                    wout_load(oc + 3)
                for (so, sn, bb) in segs:
                    stt(xT[:, oc, c0 + so:c0 + so + sn], PS[b][:, so:so + sn], modS[p][:, 16 + oc, bb:bb + 1],
                        xT[:, oc, c0 + so:c0 + so + sn], ALU.mult, ALU.add, R=[pst(b), xt] + mtok, W=[xt])
        out_ops.append(dma("sp", "omout", [(mout_d[:, 5 * l:5 * l + 5], moutS[:, 5 * l:5 * l + 5])], R=["moutS"]))

    def phase_B(l):
        p = l % 2
        mtok = [("mod", p), ("modA", p)]
        wuv = w_up_d[l].rearrange("(k p) n -> p k n", p=128)
        wdv = w_down_d[l]
        NE = 8

        def load_wu(q):
            for fc in range(4):
                f0 = (q * 4 + fc) * 128
                dma("pool", "wu%d" % fc, [(wu[:, :, fc * 128:(fc + 1) * 128], wuv[:, :, f0:f0 + 128])],
                    R=PHASE, W=[("wu", fc)])

        def load_wd(q):
            for fc in range(4):
                f0 = (q * 4 + fc) * 128
                dma("pool", "wd%d" % fc, [(wd[:, fc, :], wdv[f0:f0 + 128, :])], R=PHASE, W=[("wd", fc)])

        bigpool[0] = [0, 1, 2, 3, 4, 5, 6]
        load_wu(0)
        load_wd(0)
        if l + 1 < nl:
            modB_load(l + 1, 0)
        def emit_norm2(tt_i):
            c0, n = TILES[tt_i]
            segs = segs_of(tt_i, n)
            xt = ("x", tt_i)
            rms_rstd(tt_i, c0, n, sqB, "sqB", pp[0], "pp0")
            for kc in range(8):
                tmp = pp[1 + kc % 2]
                tk = "pp%d" % (1 + kc % 2)
                tt(tmp[:, 0:n], xT[:, kc, c0:c0 + n], pp[0][:, 0:n], ALU.mult, R=[xt, "pp0"], W=[tk])
                for (so, sn, b) in segs:
                    act(h2T[:, kc, c0 + so:c0 + so + sn], tmp[:, so:so + sn], AF.Identity, bias=modS[p][:, 24 + kc, b:b + 1],
                        scale=A2[p][:, kc, b:b + 1], R=[tk] + mtok + PHASE, W=[("h2T", tt_i)])

        def emit_up(fc, tt_i):
            c0, n = TILES[tt_i]
            b = bigbank()
            for kc in range(8):
                mm(PS[b][:, 0:n], wu[:, kc, fc * 128:(fc + 1) * 128], h2T[:, kc, c0:c0 + n], kc == 0, kc == 7,
                   R=[("wu", fc), ("h2T", tt_i)], W=[pst(b)])
            r = sbs[(fc * 5 + tt_i) % 3]
            rk = "sb%d" % ((fc * 5 + tt_i) % 3)
            act(r[:, 0:n], PS[b][:, 0:n], AF.Relu, R=[pst(b)], W=[rk])
            tt(actT[:, fc, c0:c0 + n], r[:, 0:n], r[:, 0:n], ALU.mult, R=[rk], W=[("act", fc, tt_i)])

        for q in range(NE):
            if q == 0:
                emit_norm2(0)
                for tt_i in range(len(TILES)):
                    if tt_i + 1 < len(TILES):
                        emit_norm2(tt_i + 1)
                    for fc in range(4):
                        emit_up(fc, tt_i)
            else:
                for fc in range(4):
                    for tt_i in range(len(TILES)):
                        emit_up(fc, tt_i)
            if q + 1 < NE:
                load_wu(q + 1)
            if l + 1 < nl:
                modB_compute(l + 1, q)
                if q + 1 < NE:
                    modB_load(l + 1, q + 1)
            for oc in range(8):
                for tt_i, (c0, n) in enumerate(TILES):
                    segs = segs_of(tt_i, n)
                    xt = ("x", tt_i)
                    b = bigbank()
                    for fc in range(4):
                        mm(PS[b][:, 0:n], wd[:, fc, oc * 128:(oc + 1) * 128], actT[:, fc, c0:c0 + n], fc == 0, fc == 3,
                           R=[("wd", fc), ("act", fc, tt_i)], W=[pst(b)])
                    for (so, sn, bb) in segs:
                        stt(xT[:, oc, c0 + so:c0 + so + sn], PS[b][:, so:so + sn], modS[p][:, 40 + oc, bb:bb + 1],
                            xT[:, oc, c0 + so:c0 + so + sn], ALU.mult, ALU.add, R=[pst(b), xt] + mtok, W=[xt])
            if q + 1 < NE:
                load_wd(q + 1)

    ALLTOK = (["hT", "mixT", "qT", "kT", "sigo", "k_tok", "v_tok", "swT4", "kw4", "uP", "uS", "bkv",
               ("Cbf", 0), ("Cbf", 1), ("n0rep", 0), ("n0rep", 1), ("Cbf", 2), ("Cbf", 3), ("n0rep", 2), ("n0rep", 3), "sqB", ("wadaB", 0), ("wadaB", 1), ("wadaB", 2), ("wadaB", 3), ("wadaB", 4), ("wadaB", 5), ("pq", 0), ("pq", 1), ("pq", 2), ("pq", 3), "dummy"] + [("cx", i) for i in range(20)]
              + [("win", s_) for s_ in range(nring)] + [("h2T", i) for i in range(5)]
              + [("act", f_, i) for f_ in range(4) for i in range(5)] + [("wu", f_) for f_ in range(4)]
              + [("wd", f_) for f_ in range(4)] + PHASE)
    emit_mod_blocks(0, 0, 48, use_win=True)
    for l in range(nl):
        phase_A(l)
        S.add("pool", lambda e: e.memset(Bcar[:, 0:1], 0.0), W=ALLTOK + ["Bcar"])
        phase_B(l)
        S.add("pool", lambda e: e.memset(Mcar[:, 0:1], 0.0), W=ALLTOK + ["Mcar"])
    for tt_i, (c0, n) in enumerate(TILES):
        rms_rstd(tt_i, c0, n, sqB, "sqB", pp[0], "pp0")
        for kc in range(8):
            yb = pp[1 + kc % 3]
            yk = "pp%d" % (1 + kc % 3)
            stt(yb[:, 0:n], xT[:, kc, c0:c0 + n], pf[:, NL * PFW + kc:NL * PFW + kc + 1], pp[0][:, 0:n], ALU.mult, ALU.mult,
                R=[("x", tt_i), "pp0", "pf"] + PHASE, W=[yk])
            out_ops.append(dma("sp", "oy%d" % (kc % 3), [(yT_d[:, kc, c0:c0 + n], yb[:, 0:n])], R=[yk]))
    fin = S.add("sp", lambda e: e.nop(), R=[], W=[], force=True)
    out_ops = [d for d in out_ops if id(d) not in S.dropped]
    fin.deps = list(out_ops)
    fin.dneed = {d.dkey: 16 * S.dma_count[d.dkey] for d in out_ops}
    if S.limit:
        lastop = {}
        for o in S.ops:
            if o is not fin and o.dkey is None:
                lastop[o.eng] = o
        fin.deps += list(lastop.values())
        alld = {}
        for o in S.ops:
            if o.dkey is not None:
                alld[o.dkey] = o
        fin.deps += list(alld.values())
        for o in alld.values():
            fin.dneed[o.dkey] = 16 * S.dma_count[o.dkey]
    S.emit(nc, es)
    es.close()
    return nc


_NC_CACHE = {}


def _get_nc():
    if "nc" not in _NC_CACHE:
        _NC_CACHE["nc"] = build_nc()
    return _NC_CACHE["nc"]


def _prep(x_prompt, x_sample, c_prompt, c_sample, state_C, state_n, state_m, state_conv,
          w_ada, b_ada, g_norm1, w_in, b_in, conv_w, g_mix_out, w_out, g_norm2, w_up, w_down, g_final):
    f = np.float32
    x_prompt = np.asarray(x_prompt, f); x_sample = np.asarray(x_sample, f)
    c_prompt = np.asarray(c_prompt, f); c_sample = np.asarray(c_sample, f)
    state_C = np.asarray(state_C, f); state_n = np.asarray(state_n, f)
    state_m = np.asarray(state_m, f); state_conv = np.asarray(state_conv, f)
    w_ada = np.ascontiguousarray(np.asarray(w_ada, f)); w_in = np.ascontiguousarray(np.asarray(w_in, f))
    w_out = np.ascontiguousarray(np.asarray(w_out, f)); w_up = np.ascontiguousarray(np.asarray(w_up, f))
    w_down = np.ascontiguousarray(np.asarray(w_down, f))
    b_ada = np.asarray(b_ada, f); g_norm1 = np.asarray(g_norm1, f); b_in = np.asarray(b_in, f)
    conv_w = np.asarray(conv_w, f); g_mix_out = np.asarray(g_mix_out, f); g_norm2 = np.asarray(g_norm2, f)
    g_final = np.asarray(g_final, f)

    def fm(v, nchunk):
        return v.reshape(nchunk, 128).T

    pf = np.zeros((128, NL * PFW + 8), f)
    for l in range(NL):
        o = PFW * l
        pf[:, o:o + 48] = fm(b_ada[l], 48)
        pf[:, o + 48:o + 56] = fm(g_norm1[l], 8)
        pf[:, o + 56:o + 64] = fm(g_norm2[l], 8)
        pf[:, o + 64:o + 72] = fm(g_mix_out[l], 8)
        bsel = np.concatenate([b_in[l, 0:2048], b_in[l, 2056:3592]])
        pf[:, o + 72:o + 100] = fm(bsel, 28)
        cw = conv_w[l].reshape(3, 4, 128)
        pf[:, o + 100:o + 112] = cw.transpose(2, 1, 0).reshape(128, 12)
    pf[:, NL * PFW:NL * PFW + 8] = fm(g_final, 8)
    bkv = np.ascontiguousarray(np.broadcast_to(b_in[:, None, 512:1536], (NL, 128, 1024))).astype(f)

    in_maps = []
    for i in range(8):
        xs = np.concatenate([x_prompt[i], x_sample[4 * i:4 * i + 4].reshape(64, D)], axis=0)
        xT = np.ascontiguousarray(xs.reshape(NT, 8, 128).transpose(2, 1, 0))
        cs = np.concatenate([c_prompt[i:i + 1], c_sample[4 * i:4 * i + 4]], axis=0)
        cT = np.ascontiguousarray(cs.reshape(5, 8, 128).transpose(2, 1, 0)).reshape(128, 40)
        pg = np.zeros((4, 24), f)
        for l in range(NL):
            pg[:, 2 * l] = b_in[l, 2048:2052]
            pg[:, 2 * l + 1] = b_in[l, 2052:2056]
            pg[:, 8 + 4 * l:8 + 4 * l + 4] = state_m[l, 4 * i:4 * i + 4, :].T
        sC = state_C[:, 4 * i:4 * i + 4]
        sn = state_n[:, 4 * i:4 * i + 4]
        cext = np.concatenate([sC, sn[..., None]], axis=-1)
        cst0 = np.ascontiguousarray(cext.transpose(0, 3, 1, 2, 4)).reshape(NL, 128, 16 * 129)
        sc = state_conv[:, 4 * i:4 * i + 4]
        sconv0 = np.ascontiguousarray(sc.reshape(NL, 4, 2, 4, 128).transpose(0, 4, 3, 1, 2)).reshape(NL, 128, 32)
        in_maps.append({"xT": xT, "cT": cT, "pf": pf, "pg": pg, "cst0": cst0, "sconv0": sconv0, "bkv": bkv,
                        "w_ada": w_ada, "w_in": w_in, "w_out": w_out, "w_up": w_up, "w_down": w_down})

    return in_maps


def _assemble(R):
    f = np.float32

    y_prompt = np.zeros((8, T, D), f); y_sample = np.zeros((32, 16, D), f)
    p_C = np.zeros((NL, 8, 4, 128, 128), f); p_n = np.zeros((NL, 8, 4, 128), f); p_m = np.zeros((NL, 8, 4), f)
    p_conv = np.zeros((NL, 8, 2, 512), f)
    s_C = np.zeros((NL, 32, 4, 128, 128), f); s_n = np.zeros((NL, 32, 4, 128), f); s_m = np.zeros((NL, 32, 4), f)
    s_conv = np.zeros((NL, 32, 2, 512), f)
    for i in range(8):
        r = R[i]
        y = np.asarray(r["yT"]).transpose(2, 1, 0).reshape(NT, D)
        y_prompt[i] = y[:T]
        y_sample[4 * i:4 * i + 4] = y[T:].reshape(4, 16, D)
        cst = np.asarray(r["cst"]).reshape(NL, 128, 20, 129)
        mo = np.asarray(r["mout"]).reshape(4, NL, 5)
        co = np.asarray(r["convout"]).reshape(NL, 128, 4, 5, 2)
        for l in range(NL):
            for h in range(4):
                p_C[l, i, h] = cst[l, :, h, :128]
                p_n[l, i, h] = cst[l, :, h, 128]
                p_m[l, i, h] = mo[h, l, 0]
                for s in range(4):
                    s_C[l, 4 * i + s, h] = cst[l, :, 4 + 4 * s + h, :128]
                    s_n[l, 4 * i + s, h] = cst[l, :, 4 + 4 * s + h, 128]
                    s_m[l, 4 * i + s, h] = mo[h, l, 1 + s]
            p_conv[l, i] = co[l, :, :, 0, :].transpose(2, 1, 0).reshape(2, 512)
            for s in range(4):
                s_conv[l, 4 * i + s] = co[l, :, :, 1 + s, :].transpose(2, 1, 0).reshape(2, 512)
    return (y_prompt, y_sample, p_C, p_n, p_m, p_conv, s_C, s_n, s_m, s_conv)


def kernel(**inputs):
    in_maps = _prep(**inputs)
    nc = _get_nc()
    res = run_bass_kernel_spmd(nc, in_maps, core_ids=list(range(8)))
    return _assemble(res.results)
```

```python
import numpy as np
import concourse.bass as bass
import concourse.mybir as mybir
from concourse.bass_utils import run_bass_kernel_spmd
from contextlib import ExitStack

F32 = mybir.dt.float32
BF16 = mybir.dt.bfloat16
ALU = mybir.AluOpType
AF = mybir.ActivationFunctionType

NL = 4
D = 1024
T = 2048
NSAMP = 64
NT = T + NSAMP
N_IN = 3592
DFF = 4096
EPS = 1e-6
KSCALE = 128.0 ** -0.5
TILES = [(0, 512), (512, 512), (1024, 512), (1536, 512), (2048, 64)]
NEG = -30000.0
PFW = 112


class Op:
    __slots__ = ("eng", "fn", "deps", "sig", "sigval", "dkey", "dval", "waits", "dneed", "tag")


class Sched:
    def __init__(self):
        self.ops = []
        self.lastw = {}
        self.readers = {}
        self.dma_count = {}
        self.limit = 0
        self.debug = False
        self.dropped = set()

    def add(self, eng, fn, R=(), W=(), dkey=None, ndma=0, force=False):
        op = Op()
        if self.limit and len(self.ops) >= self.limit and not force:
            op.dkey = dkey
            self.dropped.add(id(op))
            self._keep = getattr(self, "_keep", [])
            self._keep.append(op)
            return op
        op.eng = eng
        op.fn = fn
        op.dkey = dkey
        op.sig = False
        op.sigval = 0
        op.dval = 0
        deps = []
        for t in R:
            lw = self.lastw.get(t)
            if lw is not None:
                deps.append(lw)
        for t in W:
            lw = self.lastw.get(t)
            if lw is not None:
                deps.append(lw)
            rd = self.readers.get(t)
            if rd:
                deps.extend(rd.values())
        op.deps = deps
        op.dneed = {}
        if self.debug:
            import sys as _sys
            fr = _sys._getframe(1)
            tg = []
            while fr is not None and len(tg) < 3:
                tg.append(fr.f_lineno)
                fr = fr.f_back
            op.tag = tg
        for d in deps:
            if d.dkey is not None:
                op.dneed[d.dkey] = 16 * self.dma_count[d.dkey]
        for t in W:
            self.lastw[t] = op
            self.readers[t] = {}
        rk = ("d", dkey) if dkey else eng
        for t in R:
            self.readers.setdefault(t, {})[rk] = op
        if dkey:
            self.dma_count[dkey] = self.dma_count.get(dkey, 0) + ndma
            op.dval = 16 * self.dma_count[dkey]
        self.ops.append(op)
        return op

    def finalize(self):
        for op in self.ops:
            for d in op.deps:
                if d.dkey is None and not (d.eng == "pe" and op.eng == "pe"):
                    d.sig = True
        cnt = {}
        waited = {}
        for op in self.ops:
            need = {}
            for d in op.deps:
                if d is op:
                    continue
                if d.dkey is not None:
                    k = ("d", d.dkey)
                    v = op.dneed.get(d.dkey, d.dval)
                else:
                    if d.eng == "pe" and op.eng == "pe":
                        continue
                    k = ("e", d.eng)
                    v = d.sigval
                if v > need.get(k, 0):
                    need[k] = v
            w = waited.setdefault(op.eng, {})
            op.waits = []
            for k, v in need.items():
                if v > w.get(k, 0):
                    w[k] = v
                    op.waits.append((k, v))
            if op.dkey is None and op.sig:
                cnt[op.eng] = cnt.get(op.eng, 0) + 1
                op.sigval = cnt[op.eng]

    def emit(self, nc, es):
        self.finalize()
        sems = {}
        for eng in ("pe", "act", "dve", "pool", "sp"):
            sems[("e", eng)] = es.enter_context(nc.semaphore("c_" + eng))
        for key in self.dma_count:
            sems[("d", key)] = es.enter_context(nc.semaphore("d_" + key))
        block = es.enter_context(nc.Block())
        decos = (("pe", block.tensor), ("act", block.scalar), ("dve", block.vector),
                 ("pool", block.gpsimd), ("sp", block.sync))
        for eng, deco in decos:
            ops_e = [op for op in self.ops if op.eng == eng]

            def body(e, ops_e=ops_e, eng=eng):
                for op in ops_e:
                    for (k, v) in op.waits:
                        e.wait_ge(sems[k], v)
                    if op.dkey is not None:
                        op.fn(e, sems[("d", op.dkey)])
                    else:
                        inst = op.fn(e)
                        if op.sig:
                            inst.then_inc(sems[("e", eng)], 1)
            deco(body)


class Arena:
    def __init__(self, ap, nelem):
        self.ap = ap
        self.n = nelem
        self.off = 0
        self.hi = 0

    def alloc(self, nelem, dt, parts=128):
        n2 = nelem * (2 if dt == F32 else 1)
        n2 = (n2 + 15) // 16 * 16
        a = self.ap[0:parts, self.off:self.off + n2]
        self.off += n2
        self.hi = max(self.hi, self.off)
        assert self.off <= self.n, ("SBUF arena overflow", self.off, self.n)
        if dt == F32:
            a = a.bitcast(F32)
            if a.shape[-1] != nelem:
                a = a[:, 0:nelem]
        else:
            if n2 != nelem:
                a = a[:, 0:nelem]
        return a


def build_nc(nl=NL):
    nc = bass.Bass("TRN2", target_bir_lowering=False)
    S = Sched()
    import os as _os
    S.limit = int(_os.environ.get("K_LIMIT", "0"))
    S.debug = bool(_os.environ.get("K_DEBUG"))
    _NC_CACHE["S"] = S
    es = ExitStack()

    def din(name, shape):
        return nc.dram_tensor(name, shape, F32, kind="ExternalInput").ap()

    def dout(name, shape):
        return nc.dram_tensor(name, shape, F32, kind="ExternalOutput").ap()

    xT_d = din("xT", [128, 8, NT])
    cT_d = din("cT", [128, 40])
    pf_d = din("pf", [128, NL * PFW + 8])
    pg_d = din("pg", [4, 24])
    cst0_d = din("cst0", [NL, 128, 16 * 129])
    sconv0_d = din("sconv0", [NL, 128, 32])
    bkv_d = din("bkv", [NL, 128, 1024])
    w_ada_d = din("w_ada", [NL, D, 6 * D])
    w_in_d = din("w_in", [NL, D, N_IN])
    w_out_d = din("w_out", [NL, D, D])
    w_up_d = din("w_up", [NL, D, DFF])
    w_down_d = din("w_down", [NL, DFF, D])
    yT_d = dout("yT", [128, 8, NT])
    cst_d = dout("cst", [NL, 128, 20 * 129])
    mout_d = dout("mout", [4, NL * 5])
    convout_d = dout("convout", [NL, 128, 40])

    ARENA_ELEMS = 212800 // 2
    arena_t = es.enter_context(nc.sbuf_tensor("arena", [128, ARENA_ELEMS], BF16))
    AR = Arena(arena_t, ARENA_ELEMS)
    PS = [es.enter_context(nc.psum_tensor("psb%d" % i, [128, 512], F32)) for i in range(8)]

    def pst(i):
        return ("ps", i)

    xT = AR.alloc(8 * NT, F32).rearrange("p (k t) -> p k t", k=8)
    pf = AR.alloc(NL * PFW + 8, F32)
    pg = AR.alloc(24, F32, parts=4)
    cTs = AR.alloc(40, F32).rearrange("p (k b) -> p k b", k=8)
    scb = AR.alloc(40, BF16).rearrange("p (k b) -> p k b", k=8)
    ones_bf = AR.alloc(128, BF16)
    ones4w = AR.alloc(512, F32, parts=4)
    SEL = AR.alloc(512, F32, parts=4).rearrange("p (h m) -> p h m", h=4)
    rAab = AR.alloc(512, BF16, parts=4)
    pgn = AR.alloc(16, F32, parts=4)
    Mcarn = AR.alloc(1, F32, parts=4)
    modS = [AR.alloc(240, F32).rearrange("p (c b) -> p c b", c=48) for _ in range(2)]
    A1 = [AR.alloc(40, F32).rearrange("p (c b) -> p c b", c=8) for _ in range(2)]
    A2 = [AR.alloc(40, F32).rearrange("p (c b) -> p c b", c=8) for _ in range(2)]
    bks = AR.alloc(NL * 4, F32)
    negbf = AR.alloc(NL, F32, parts=4)
    Bcar = AR.alloc(1, F32, parts=4)
    Mcar = AR.alloc(1, F32, parts=4)
    moutS = AR.alloc(NL * 5, F32, parts=4)
    dec = AR.alloc(32, F32).rearrange("p (h c) -> p h c", h=4)
    wada_ring = None
    wout_ring = [AR.alloc(8 * 128, BF16).rearrange("p (k n) -> p k n", k=8) for _ in range(3)]
    pp = [AR.alloc(512, F32) for _ in range(4)]
    sbs = [AR.alloc(512, BF16) for _ in range(4)]
    qgs = [AR.alloc(512, BF16) for _ in range(4)]
    rows = [AR.alloc(512, F32, parts=4) for _ in range(5)]
    rGb = AR.alloc(512, BF16, parts=4)
    rEb = AR.alloc(512, BF16, parts=4)
    SELb = AR.alloc(512, BF16, parts=4).rearrange("p (h m) -> p h m", h=4)
    maskPb = AR.alloc(256, BF16).rearrange("p (h m) -> p h m", h=4)
    maskSb = AR.alloc(256, BF16).rearrange("p (h m) -> p h m", h=4)
    mark = AR.off

    hT_flat = AR.alloc(8 * 512, BF16)
    hT = hT_flat.rearrange("p (k t) -> p k t", k=8)
    pq = [hT_flat[:, 1024 * i:1024 * (i + 1)].bitcast(F32) for i in range(4)]
    dummy = AR.alloc(16, F32)
    mixT = AR.alloc(8 * 512, BF16).rearrange("p (k t) -> p k t", k=8)
    AR_qk_flat = AR.alloc(8 * 512, BF16)
    AR_qk = AR_qk_flat.rearrange("p (k t) -> p k t", k=8)
    qT = AR_qk_flat[:, 0:2048].rearrange("p (k t) -> p k t", k=4)
    kT = AR_qk_flat[:, 2048:4096].rearrange("p (k t) -> p k t", k=4)
    sigo = AR.alloc(4 * 512, BF16).rearrange("p (k t) -> p k t", k=4)
    k_tok = AR.alloc(4 * 512, BF16).rearrange("p (s h d) -> p s h d", s=4, h=4)
    v_tok = AR.alloc(4 * 4 * 130, BF16).rearrange("p (s h d) -> p s h d", s=4, h=4)
    swT4 = AR.alloc(4 * 256, BF16).rearrange("p (s h t) -> p s h t", s=4, h=4)
    kw4 = AR.alloc(4 * 512, BF16).rearrange("p (s h d) -> p s h d", s=4, h=4)
    uP = AR.alloc(4 * 514, BF16).rearrange("p (j t) -> p j t", j=4)
    uS = AR.alloc(4 * 4 * 18, BF16).rearrange("p (j s t) -> p j s t", j=4, s=4)
    CextP = AR.alloc(4 * 129, F32).rearrange("p (h e) -> p h e", h=4)
    CextS = AR.alloc(16 * 129, F32).rearrange("p (g e) -> p g e", g=16)
    Cbf = [AR.alloc(128, BF16) for _ in range(4)]
    n0rep = [AR.alloc(128, BF16) for _ in range(4)]
    bkvS = AR.alloc(1024, BF16)
    WSLOT = 8 * 520
    nring = 4 if (AR.n - AR.off) >= 4 * WSLOT + 64 else (3 if (AR.n - AR.off) >= 3 * WSLOT + 64 else 2)
    win_ring = [AR.alloc(WSLOT, BF16).rearrange("p (k n) -> p k n", k=8) for _ in range(nring)]
    endA = AR.off
    AR.off = mark
    h2T = AR.alloc(8 * NT, BF16).rearrange("p (k t) -> p k t", k=8)
    actT = AR.alloc(4 * NT, BF16).rearrange("p (f t) -> p f t", f=4)
    wu = AR.alloc(8 * 512, BF16).rearrange("p (k n) -> p k n", k=8)
    wd = AR.alloc(4 * 1024, BF16).rearrange("p (f n) -> p f n", f=4)
    sqB = AR.alloc(8 * 512, BF16).rearrange("p (k t) -> p k t", k=8)
    wadaB = [AR.alloc(8 * 128, BF16).rearrange("p (k n) -> p k n", k=8) for _ in range(6)]
    endB = AR.off
    AR.off = max(endA, endB)
    PHASE = ["phaseAB"]

    def act(out, in_, func, bias=0.0, scale=1.0, R=(), W=()):
        return S.add("act", lambda e: e.activation(out=out, in_=in_, func=func, bias=bias, scale=scale), R, W)

    def tt(out, in0, in1, op, R=(), W=(), eng="dve"):
        return S.add(eng, lambda e: e.tensor_tensor(out=out, in0=in0, in1=in1, op=op), R, W)

    def ts(out, in0, s1, s2, op0, op1=None, R=(), W=(), eng="dve"):
        if op1 is None:
            return S.add(eng, lambda e: e.tensor_scalar(out=out, in0=in0, scalar1=s1, scalar2=None, op0=op0), R, W)
        return S.add(eng, lambda e: e.tensor_scalar(out=out, in0=in0, scalar1=s1, scalar2=s2, op0=op0, op1=op1), R, W)

    def stt(out, in0, scalar, in1, op0, op1, R=(), W=()):
        return S.add("dve", lambda e: e.scalar_tensor_tensor(out=out, in0=in0, scalar=scalar, in1=in1, op0=op0, op1=op1), R, W)

    def cpy(out, in_, R=(), W=(), eng="dve"):
        return S.add(eng, lambda e: e.tensor_copy(out=out, in_=in_), R, W)

    def mset(ap, val, W=(), eng="pool"):
        return S.add(eng, lambda e: e.memset(ap, val), (), W)

    def mm(out, lhsT, rhs, start, stop, R=(), W=()):
        return S.add("pe", lambda e: e.matmul(out, lhsT=lhsT, rhs=rhs, start=start, stop=stop), R, W)

    def scan(out, d0, d1, init, op0, op1, R=(), W=()):
        return S.add("dve", lambda e: e.tensor_tensor_scan(out=out, data0=d0, data1=d1, initial=init, op0=op0, op1=op1), R, W)

    def dma(eng, key, pairs, R=(), W=(), cast=False):
        def fn(e, sem):
            for (o, i) in pairs:
                e.dma_start(out=o, in_=i).then_inc(sem, 16)
        return S.add(eng, fn, R, W, dkey=key, ndma=len(pairs))

    bigc = [0]
    bigpool = [[0, 1]]

    def bigbank():
        pool_ = bigpool[0]
        b = pool_[bigc[0] % len(pool_)]
        bigc[0] += 1
        return b

    out_ops = []

    for tt_i, (c0, n) in enumerate(TILES):
        dma("sp", "xin%d" % tt_i, [(xT[:, :, c0:c0 + n], xT_d[:, :, c0:c0 + n])], W=[("x", tt_i)])
    dma("sp", "pf", [(pf, pf_d[:, :])], W=["pf"])
    dma("sp", "pg", [(pg, pg_d[:, :])], W=["pg"])
    dma("sp", "cT", [(cTs, cT_d[:, :].rearrange("p (k b) -> p k b", k=8))], W=["cT"])
    mset(ones_bf, 1.0, W=["ones_bf"])
    mset(ones4w, 1.0, W=["ones4w"])
    mset(pp[0], 0.0, W=["pp0"])
    mset(pp[1], 1.0, W=["pp1"])
    ones4v = ones4w[:, 0:128]
    for h in range(4):
        S.add("pool", lambda e, h=h: e.affine_select(out=SEL[:, h, :], in_=ones4v, pattern=[[0, 128]], base=-h,
                                                      channel_multiplier=1, compare_op=ALU.is_equal, fill=0.0),
              R=["ones4w"], W=["SEL"])
    ts(pgn, pg[:, 8:24], -1.0, None, ALU.mult, R=["pg"], W=["pgn"])
    for hf in range(2):
        o64 = pp[1][64 * hf:64 * hf + 64, 0:256].rearrange("p (h m) -> p h m", h=4)
        mP = maskPb[64 * hf:64 * hf + 64]
        mS = maskSb[64 * hf:64 * hf + 64]
        S.add("pool", lambda e, o64=o64, mP=mP: e.affine_select(out=mP, in_=o64, pattern=[[0, 4], [1, 64]], base=0,
                                                                  channel_multiplier=-1, compare_op=ALU.is_ge, fill=0.0),
              R=["pp1"], W=["maskPb"])
        S.add("pool", lambda e, o64=o64, mS=mS: e.affine_select(out=mS, in_=o64, pattern=[[0, 4], [1, 64]], base=0,
                                                                  channel_multiplier=-1, compare_op=ALU.is_ge, fill=0.0),
              R=["pp1"], W=["maskSb"])
        for j in range(1, 4):
            S.add("pool", lambda e, j=j, mS=mS: e.affine_select(out=mS[:, :, 16 * j:16 * j + 16], in_=mS[:, :, 16 * j:16 * j + 16],
                                                                  pattern=[[0, 4], [0, 16]], base=-16 * j, channel_multiplier=1,
                                                                  compare_op=ALU.is_ge, fill=0.0),
                  R=["maskSb"], W=["maskSb"])
    act(scb, cTs, AF.Silu, R=["cT"], W=["scb"])
    cpy(SELb, SEL, R=["SEL"], W=["SELb"])
    for l in range(NL):
        ts(bks[:, 4 * l:4 * l + 4], pf[:, PFW * l + 72 + 4:PFW * l + 72 + 8], KSCALE, None, ALU.mult, R=["pf"], W=["bks"])
        ts(negbf[:, l:l + 1], pg[:, 2 * l + 1:2 * l + 2], -1.0, None, ALU.mult, R=["pg"], W=["negbf"])

    def pfc(l, off, k=None):
        base = PFW * l + off
        if k is None:
            return pf[:, base:base + 1]
        return pf[:, base + k:base + k + 1]

    MODBANK = 7
    wada_cnt = [0]

    def emit_mod_blocks(l, cb0, cb1, use_win=False):
        wv = w_ada_d[l].rearrange("(k p) n -> p k n", p=128)
        if use_win:
            for g in range(cb0 // 4, cb1 // 4):
                slot = win_cnt[0] % nring
                win_cnt[0] += 1
                dma("pool", "win%d" % slot, [(win_ring[slot][:, :, 0:512], wv[:, :, g * 512:(g + 1) * 512])], W=[("win", slot)])
                for c4 in range(4):
                    cb = 4 * g + c4
                    for kc in range(8):
                        mm(PS[MODBANK][:, cb * 5:cb * 5 + 5], win_ring[slot][:, kc, c4 * 128:(c4 + 1) * 128], scb[:, kc, :],
                           kc == 0, kc == 7, R=[("win", slot), "scb"], W=[pst(MODBANK)])
        for cb in (range(cb0, cb1) if not use_win else ()):
            slot = wada_cnt[0] % 2
            wada_cnt[0] += 1
            dma("pool", "wada%d" % slot, [(wada_ring[slot], wv[:, :, cb * 128:(cb + 1) * 128])], W=[("wada", slot)])
            for kc in range(8):
                mm(PS[MODBANK][:, cb * 5:cb * 5 + 5], wada_ring[slot][:, kc, :], scb[:, kc, :], kc == 0, kc == 7,
                   R=[("wada", slot), "scb"], W=[pst(MODBANK)])
        if cb1 == 48:
            p = l % 2
            ms = modS[p]
            tt(ms, PS[MODBANK][:, 0:240].rearrange("p (c b) -> p c b", c=48),
               pf[:, PFW * l:PFW * l + 48].rearrange("p (c o) -> p c o", o=1).to_broadcast([128, 48, 5]), ALU.add,
               R=[pst(MODBANK), "pf"], W=[("mod", p)])
            g1b = pf[:, PFW * l + 48:PFW * l + 56].rearrange("p (c o) -> p c o", o=1).to_broadcast([128, 8, 5])
            g2b = pf[:, PFW * l + 56:PFW * l + 64].rearrange("p (c o) -> p c o", o=1).to_broadcast([128, 8, 5])
            stt(A1[p], ms[:, 8:16, :], 1.0, g1b, ALU.add, ALU.mult, R=[("mod", p), "pf"], W=[("modA", p)])
            stt(A2[p], ms[:, 32:40, :], 1.0, g2b, ALU.add, ALU.mult, R=[("mod", p), "pf"], W=[("modA", p)])

    def modB_load(l, q):
        wv = w_ada_d[l].rearrange("(k p) n -> p k n", p=128)
        for i in range(6):
            cb = 6 * q + i
            dma("pool", "wadaB%d" % i, [(wadaB[i], wv[:, :, cb * 128:(cb + 1) * 128])], R=PHASE, W=[("wadaB", i)])

    def modB_compute(l, q):
        for i in range(6):
            cb = 6 * q + i
            for kc in range(8):
                mm(PS[MODBANK][:, cb * 5:cb * 5 + 5], wadaB[i][:, kc, :], scb[:, kc, :], kc == 0, kc == 7,
                   R=[("wadaB", i), "scb"], W=[pst(MODBANK)])
        if q == 7:
            emit_mod_blocks(l, 48, 48)

    def rms_rstd(tt_i, c0, n, sq8, sq_tok, dst, dst_tok, extra=()):
        act(sq8[:, :, 0:n], xT[:, :, c0:c0 + n], AF.Square, R=[("x", tt_i)] + PHASE, W=[sq_tok] + list(extra))
        b = bigbank()
        for kc in range(8):
            mm(PS[b][:, 0:n], ones_bf, sq8[:, kc, 0:n], kc == 0, kc == 7, R=[sq_tok, "ones_bf"] + list(extra), W=[pst(b)])
        act(dst[:, 0:n], PS[b][:, 0:n], AF.Ln, bias=EPS, scale=1.0 / D, R=[pst(b)], W=[dst_tok])
        act(dst[:, 0:n], dst[:, 0:n], AF.Exp, scale=-0.5, R=[dst_tok], W=[dst_tok])

    def segs_of(tt_i, n):
        if tt_i < 4:
            return [(0, n, 0)]
        return [(16 * j, 16, 1 + j) for j in range(4)]

    win_cnt = [0]
    wout_cnt = [0]

    def phase_A(l):
        p = l % 2
        mtok = [("mod", p), ("modA", p)]
        wv = w_in_d[l].rearrange("(k p) n -> p k n", p=128)
        wov = w_out_d[l].rearrange("(k p) n -> p k n", p=128)
        dma("pool", "bkv", [(bkvS, bkv_d[l])], R=PHASE, W=["bkv"])
        mset(CextP, 0.0, W=[("cx", h) for h in range(4)], eng="pool")
        dma("sp", "cst0", [(CextS, cst0_d[l].rearrange("p (g e) -> p g e", g=16))], W=[("cx", 4 + g) for g in range(16)])
        mset(uP[:, :, 0:2], 0.0, W=["uP"], eng="pool")
        dma("pool", "sconv0", [(uS[:, :, :, 0:2], sconv0_d[l].rearrange("p (j s r) -> p j s r", j=4, s=4))], W=["uS"])
        mset(v_tok[:, :, :, 128:129], 1.0, W=["v_tok"], eng="pool")

        blocks = [("OG", [(1536, 520)]), ("V", [(1024, 512)])]
        for j in range(4):
            blocks.append(("C%d" % j, [(2056 + 128 * j, 128), (2568 + 128 * j, 128), (3080 + 128 * j, 128)]))
        blocks += [("Q", [(0, 512)]), ("K", [(512, 512)])]

        nb = len(blocks)
        total_blocks = nb * len(TILES)
        issued = [0]
        slot_of = {}

        def ensure_issued(g):
            while issued[0] <= g and issued[0] < total_blocks:
                gi = issued[0]
                bi_ = gi % nb
                slot = win_cnt[0] % nring
                win_cnt[0] += 1
                pairs = []
                o = 0
                for (cs, cn) in blocks[bi_][1]:
                    pairs.append((win_ring[slot][:, :, o:o + cn], wv[:, :, cs:cs + cn]))
                    o += cn
                dma("pool", "win%d" % slot, pairs, W=[("win", slot)])
                slot_of[gi] = slot
                issued[0] += 1

        conv_tail = [None]
        sq8A = AR_qk

        def emit_norm1(ti):
            c0_, n_ = TILES[ti]
            segs_ = segs_of(ti, n_)
            xt_ = ("x", ti)
            rms_rstd(ti, c0_, n_, sq8A, "qkT", pp[0], "pp0", extra=["qT", "kT"])
            for kc in range(8):
                tmp = pp[1 + kc % 2]
                tk = "pp%d" % (1 + kc % 2)
                tt(tmp[:, 0:n_], xT[:, kc, c0_:c0_ + n_], pp[0][:, 0:n_], ALU.mult, R=[xt_, "pp0"], W=[tk])
                for (so, sn, b) in segs_:
                    act(hT[:, kc, so:so + sn], tmp[:, so:so + sn], AF.Identity, bias=modS[p][:, kc, b:b + 1],
                        scale=A1[p][:, kc, b:b + 1], R=[tk] + mtok, W=["hT"])

        for tt_i, (c0, n) in enumerate(TILES):
            segs = segs_of(tt_i, n)
            nsub = max(1, n // 128)
            subP = 128 if n >= 128 else n
            nch = max(1, n // 64)
            xt = ("x", tt_i)
            bigpool[0] = [0, 1, 2, 5, 6, 7, 3, 4]
            if tt_i == 0:
                emit_norm1(0)
            rA, rL, rB, rM, rG = rows
            for bi in range(nb):
                G = tt_i * nb + bi
                ensure_issued(G + nring - 1)
                slot = slot_of[G]
                wsl = win_ring[slot]
                wtk = ("win", slot)
                name = blocks[bi][0]

                def fm_group(col, M, evac):
                    b = bigbank()
                    for kc in range(8):
                        mm(PS[b][0:M, 0:n], wsl[:, kc, col:col + M], hT[:, kc, 0:n], kc == 0, kc == 7,
                           R=[wtk, "hT"], W=[pst(b)])
                    evac(PS[b][0:M, 0:n], pst(b))

                if name == "OG":
                    for j in range(4):
                        fm_group(128 * j, 128, lambda ps, ptk, j=j: act(sigo[:, j, 0:n], ps, AF.Sigmoid, bias=pfc(l, 72, 12 + j),
                                                                        R=[ptk, "pf"], W=["sigo"]))
                    fm_group(512, 4, lambda ps, ptk: act(rA[:, 0:n], ps, AF.Identity, bias=pg[:, 2 * l:2 * l + 1],
                                                         R=[ptk, "pg"], W=["rA"]))

                    def evF(ps, ptk):
                        act(rL[:, 0:n], ps, AF.Exp, bias=negbf[:, l:l + 1], scale=-1.0, R=[ptk, "negbf"], W=["rL"])
                        act(rL[:, 0:n], rL[:, 0:n], AF.Ln, bias=1.0, scale=1.0, R=["rL"], W=["rL"])
                    fm_group(516, 4, evF)
                    for si, (so, sn, b) in enumerate(segs):
                        if tt_i == 4:
                            initB = 0.0
                            initM = pg[:, 8 + 4 * l + si:8 + 4 * l + si + 1]
                        elif tt_i == 0:
                            initB = 0.0
                            initM = 0.0
                        else:
                            initB = Bcar[:, 0:1]
                            initM = Mcar[:, 0:1]
                        scan(rB[:, so:so + sn], ones4w[:, 0:sn], rL[:, so:so + sn], initB, ALU.mult, ALU.add,
                             R=["rL", "ones4w", "Bcar"], W=["rB"])
                        tt(rA[:, so:so + sn], rA[:, so:so + sn], rB[:, so:so + sn], ALU.add, R=["rA", "rB"], W=["rA"])
                        scan(rM[:, so:so + sn], ones4w[:, 0:sn], rA[:, so:so + sn], initM, ALU.mult, ALU.max,
                             R=["rA", "ones4w", "Mcar", "pg"], W=["rM"])
                        if tt_i == 4:
                            act(rG[:, so:so + sn], rM[:, so:so + sn], AF.Exp, bias=initM, scale=-1.0, R=["rM", "pg"], W=["rG"])
                        else:
                            for ci in range(nch):
                                cc = 64 * ci
                                if ci == 0:
                                    bias = initM
                                else:
                                    bias = rM[:, cc - 1:cc]
                                act(rG[:, cc:cc + 64], rM[:, cc:cc + 64], AF.Exp, bias=bias, scale=-1.0,
                                    R=["rM", "Mcar"], W=["rG"])
                        if tt_i == 3 or tt_i == 4:
                            e_ = so + sn
                            tt(moutS[:, 5 * l + b:5 * l + b + 1], rM[:, e_ - 1:e_], rB[:, e_ - 1:e_], ALU.subtract,
                               R=["rM", "rB"], W=["moutS"])
                    if tt_i < 3:
                        cpy(Bcar[:, 0:1], rB[:, n - 1:n], R=["rB"], W=["Bcar"])
                    tt(rL[:, 0:n], rB[:, 0:n], rM[:, 0:n], ALU.subtract, R=["rB", "rM"], W=["rL"])
                    act(rEb[:, 0:n], rL[:, 0:n], AF.Exp, R=["rL"], W=["rEb"])
                    cpy(rGb[:, 0:n], rG[:, 0:n], R=["rG"], W=["rGb"])
                    ts(rL[:, 0:n], rM[:, 0:n], -1.0, None, ALU.mult, R=["rM", "rEb"], W=["rL"])
                    if tt_i == 4:
                        for si, (so, sn, b) in enumerate(segs):
                            act(rAab[:, so:so + sn], rA[:, so:so + sn], AF.Exp, bias=pgn[:, 4 * l + si:4 * l + si + 1],
                                R=["rA", "pgn"], W=["rAab"])
                            act(rB[:, so:so + sn], rA[:, so:so + sn], AF.Exp, bias=rL[:, so + sn - 1:so + sn],
                                R=["rA", "rL", "moutS", "Bcar"], W=["rB"])
                    else:
                        for ci in range(nch):
                            cc = 64 * ci
                            if ci == 0:
                                bias = 0.0 if tt_i == 0 else Mcarn[:, 0:1]
                            else:
                                bias = rL[:, cc - 1:cc]
                            act(rAab[:, cc:cc + 64], rA[:, cc:cc + 64], AF.Exp, bias=bias, R=["rA", "rL", "Mcarn"], W=["rAab"])
                            act(rB[:, cc:cc + 64], rA[:, cc:cc + 64], AF.Exp, bias=rL[:, cc + 63:cc + 64],
                                R=["rA", "rL", "moutS", "Bcar"], W=["rB"])
                    if tt_i < 3:
                        cpy(Mcar[:, 0:1], rM[:, n - 1:n], R=["rM"], W=["Mcar"])
                        ts(Mcarn[:, 0:1], rM[:, n - 1:n], -1.0, None, ALU.mult, R=["rM", "rAab"], W=["Mcarn"])
                elif name == "Q":
                    for j in range(4):
                        gbS = sbs[2 + j % 2]
                        gbk = "sb%d" % (2 + j % 2)
                        b_ = bigbank()
                        mm(PS[b_][:, 0:n], SELb[:, j, :], rGb[:, 0:n], True, True, R=["rGb", "SELb"], W=[pst(b_)])
                        act(gbS[:, 0:n], PS[b_][:, 0:n], AF.Copy, R=[pst(b_)], W=[gbk])
                        fm_group(128 * j, 128, lambda ps, ptk, j=j, gbS=gbS, gbk=gbk: stt(
                            qgs[j][:, 0:n], ps, pfc(l, 72, j), gbS[:, 0:n], ALU.add, ALU.mult,
                            R=[ptk, "pf", gbk], W=[("qg", j)]))
                        if j == 0 and conv_tail[0] is not None:
                            conv_tail[0]()
                            conv_tail[0] = None
                        bd = bigbank()
                        if tt_i < 4:
                            mm(PS[bd][:, 0:nch], SEL[:, j, :], rG[:, 63:n:64], True, True, R=["rG", "SEL"], W=[pst(bd)])
                            cpy(dec[:, j, 0:nch], PS[bd][:, 0:nch], R=[pst(bd)], W=[("dec", j)])
                        else:
                            mm(PS[bd][:, 0:4], SEL[:, j, :], rG[:, 15:64:16], True, True, R=["rG", "SEL"], W=[pst(bd)])
                            cpy(dec[:, j, 0:4], PS[bd][:, 0:4], R=[pst(bd)], W=[("dec", j)])
                elif name in ("K", "V"):
                    if name == "K":
                        for j in range(4):
                            abS = sbs[2 + j % 2]
                            abk = "sb%d" % (2 + j % 2)
                            b_ = bigbank()
                            mm(PS[b_][:, 0:n], SELb[:, j, :], rAab[:, 0:n], True, True, R=["rAab", "SELb"], W=[pst(b_)])
                            act(abS[:, 0:n], PS[b_][:, 0:n], AF.Copy, scale=KSCALE, R=[pst(b_)], W=[abk])
                            fm_group(128 * j, 128, lambda ps, ptk, j=j, abS=abS, abk=abk: stt(
                                kT[:, j, 0:n], ps, pfc(l, 72, 4 + j), abS[:, 0:n], ALU.add, ALU.mult,
                                R=[ptk, "pf", abk], W=["kT"]))
                    for st in range(nsub):
                        b = bigbank()
                        for kc in range(8):
                            mm(PS[b][0:subP, :], hT[:, kc, st * 128:st * 128 + subP], wsl[:, kc, 0:512], kc == 0, kc == 7,
                               R=[wtk, "hT"], W=[pst(b)])
                        psv = PS[b][0:subP, :].rearrange("p (h d) -> p h d", h=4)
                        if name == "K":
                            tt(k_tok[0:subP, st, :, :], psv, bkvS[0:subP, 0:512].rearrange("p (h d) -> p h d", h=4),
                               ALU.add, R=[pst(b), "bkv"], W=["k_tok"])
                        else:
                            tt(v_tok[0:subP, st, :, 0:128], psv, bkvS[0:subP, 512:1024].rearrange("p (h d) -> p h d", h=4),
                               ALU.add, R=[pst(b), "bkv"], W=["v_tok"])
                else:
                    j = int(name[1])
                    cTj, bgj = sbs[0], sbs[1]
                    sqj = sbs[2 + j % 2]
                    sqk = "sb%d" % (2 + j % 2)
                    if conv_tail[0] is not None:
                        prev_tail = conv_tail[0]
                        conv_tail[0] = None
                    else:
                        prev_tail = None
                    fm_group(0, 128, lambda ps, ptk: act(bgj[:, 0:n], ps, AF.Identity, bias=pfc(l, 72, 16 + j),
                                                         R=[ptk, "pf"], W=["sb1"]))
                    fm_group(128, 128, lambda ps, ptk: act(cTj[:, 0:n], ps, AF.Identity, bias=pfc(l, 72, 20 + j),
                                                           R=[ptk, "pf"], W=["sb0"]))
                    if tt_i < 4:
                        uw = uP[:, j, 2:2 + n]
                        utk = "uP"
                        ctv = cTj[:, 0:n]
                    else:
                        uw = uS[:, j, :, 2:18]
                        utk = "uS"
                        ctv = cTj[:, 0:n].rearrange("p (s t) -> p s t", s=4)

                    def evX(ps, ptk):
                        pv = ps if tt_i < 4 else ps.rearrange("p (s t) -> p s t", s=4)
                        stt(uw, pv, pfc(l, 72, 24 + j), ctv, ALU.add, ALU.mult, R=[ptk, "pf", "sb0"], W=[utk])
                    fm_group(256, 128, evX)
                    if prev_tail is not None:
                        prev_tail()
                    acc, ybg, rr = pp[1], pp[2 + j % 2], pp[0]
                    ybk = "pp%d" % (2 + j % 2)
                    if tt_i < 4:
                        taps = [uP[:, j, k:k + n] for k in range(3)]
                        accv, ybgv, bgv = acc[:, 0:n], ybg[:, 0:n], bgj[:, 0:n]
                    else:
                        taps = [uS[:, j, :, k:k + 16] for k in range(3)]
                        accv = acc[:, 0:n].rearrange("p (s t) -> p s t", s=4)
                        ybgv = ybg[:, 0:n].rearrange("p (s t) -> p s t", s=4)
                        bgv = bgj[:, 0:n].rearrange("p (s t) -> p s t", s=4)
                    ts(accv, taps[0], pfc(l, 100, 3 * j + 0), None, ALU.mult, R=[utk, "pf"], W=["pp1"])
                    stt(accv, taps[1], pfc(l, 100, 3 * j + 1), accv, ALU.mult, ALU.add, R=[utk, "pf", "pp1"], W=["pp1"])
                    stt(accv, taps[2], pfc(l, 100, 3 * j + 2), accv, ALU.mult, ALU.add, R=[utk, "pf", "pp1"], W=["pp1"])
                    tt(ybgv, accv, bgv, ALU.mult, R=["pp1", "sb1"], W=[ybk])
                    act(sqj[:, 0:n], ybg[:, 0:n], AF.Square, R=[ybk], W=[sqk])

                    def tail(j=j, sqj=sqj, sqk=sqk, ybg=ybg, ybk=ybk, rr=rr):
                        b = bigbank()
                        mm(PS[b][:, 0:n], ones_bf, sqj[:, 0:n], True, True, R=[sqk, "ones_bf"], W=[pst(b)])
                        act(rr[:, 0:n], PS[b][:, 0:n], AF.Ln, bias=EPS, scale=1.0 / 128, R=[pst(b)], W=["pp0"])
                        act(rr[:, 0:n], rr[:, 0:n], AF.Exp, scale=-0.5, R=["pp0"], W=["pp0"])
                        stt(mixT[:, 4 + j, 0:n], ybg[:, 0:n], pfc(l, 64, 4 + j), rr[:, 0:n], ALU.mult, ALU.mult,
                            R=[ybk, "pp0", "pf"], W=["mixT"])
                    conv_tail[0] = tail
                    if j == 3:
                        if tt_i < 3:
                            cpy(uP[:, :, 0:2], uP[:, :, 512:514], R=["uP"], W=["uP"])
                        elif tt_i == 3:
                            out_ops.append(dma("pool", "oconv", [(convout_d[l].rearrange("p (j s r) -> p j s r", j=4, s=5)[:, :, 0, :],
                                                                   uP[:, :, 512:514])], R=["uP"]))
                        else:
                            out_ops.append(dma("pool", "oconv", [(convout_d[l].rearrange("p (j s r) -> p j s r", j=4, s=5)[:, :, 1:5, :],
                                                                   uS[:, :, :, 16:18])], R=["uS"]))

            wout_slot = {}

            def wout_load(oc):
                slot = wout_cnt[0] % 3
                wout_cnt[0] += 1
                dma("pool", "wout%d" % slot, [(wout_ring[slot], wov[:, :, oc * 128:(oc + 1) * 128])], W=[("wout", slot)])
                wout_slot[oc] = slot
            for oc in range(3):
                wout_load(oc)
            if conv_tail[0] is not None:
                conv_tail[0]()
                conv_tail[0] = None
            bigpool[0] = [0, 1]
            mask = maskPb if tt_i < 4 else maskSb
            for ci in range(nch):
                cc = 64 * ci
                half = ci % 2
                st = ci // 2
                P0 = 64 * half
                lb = 3 + (ci % 2)
                ST = PS[lb][P0:P0 + 64, 0:256]
                WC = PS[lb][P0:P0 + 64, 256:260]
                for h in range(4):
                    mm(ST[:, 64 * h:64 * h + 64], kT[:, h, cc:cc + 64], qgs[h][:, cc:cc + 64], True, True,
                       R=["kT", ("qg", h)], W=[pst(lb)])
                mm(WC, rB[:, cc:cc + 64], SEL[:, :, 0:1], True, True, R=["rB", "SEL"], W=[pst(lb)])
                tt(swT4[P0:P0 + 64, st, :, :], ST.rearrange("p (h t) -> p h t", h=4), mask[P0:P0 + 64], ALU.mult,
                   R=[pst(lb), "maskPb", "maskSb"], W=["swT4"])
                wcb = WC.rearrange("p (h o) -> p h o", o=1)
                if tt_i < 4:
                    stt(kw4[P0:P0 + 64, st, :, :], k_tok[P0:P0 + 64, st, :, :], KSCALE,
                        wcb.to_broadcast([64, 4, 128]), ALU.mult, ALU.mult, R=["k_tok", pst(lb)], W=["kw4"])
                else:
                    for g in range(4):
                        wg = sbs[0][0:64, 4 * g:4 * g + 4]
                        tt(wg, WC, mask[0:64, :, 16 * g + 15], ALU.mult, R=[pst(lb), "maskSb"], W=["sb0"])
                        stt(kw4[0:64, g, :, :], k_tok[0:64, 0, :, :], KSCALE,
                            wg.rearrange("p (h o) -> p h o", o=1).to_broadcast([64, 4, 128]), ALU.mult, ALU.mult,
                            R=["k_tok", "sb0"], W=["kw4"])

            NUMBs, DENBs, CUBs = [5, 3], [6, 4], [2, 7]

            units = []
            for ci in range(nch):
                cc = 64 * ci
                st = ci // 2
                P0 = 64 * (ci % 2)
                if tt_i < 4:
                    groups = [(cc, 64, None, st, ci)]
                else:
                    groups = [(16 * g, 16, g, g, g) for g in range(4)]
                for gi, (gc, gn, sg, kslot, dci) in enumerate(groups):
                    units.append(dict(ci=ci, cc=cc, st=st, P0=P0, gc=gc, gn=gn, sg=sg, kslot=kslot, dci=dci,
                                      first=(gi == 0), last=(gi == len(groups) - 1)))

            def emit_cupd(ui, h):
                u = units[ui]
                P0, st = u["P0"], u["st"]
                cb_ = CUBs[h % 2]
                mm(PS[cb_][:, 0:129], kw4[P0:P0 + 64, u["kslot"], h, :], v_tok[P0:P0 + 64, st, h, 0:129], True, True,
                   R=["kw4", "v_tok"], W=[pst(cb_)])

            def emit_unit(ui, h):
                u = units[ui]
                P0, st, cc = u["P0"], u["st"], u["cc"]
                sidx = h if u["sg"] is None else 4 + 4 * u["sg"] + h
                cx = CextP[:, sidx, :] if sidx < 4 else CextS[:, sidx - 4, :]
                ctk = ("cx", sidx)
                cb_ = CUBs[h % 2]
                NUMB, DENB = NUMBs[h % 2], DENBs[h % 2]
                sc = 2 * (h % 2) + ui % 2
                qg = qgs[h]
                qtk = ("qg", h)
                act(Cbf[sc], cx[:, 0:128], AF.Copy, R=[ctk], W=[("Cbf", sc)])
                ts(n0rep[sc], ones_bf, cx[:, 128:129], 1.0, ALU.mult, ALU.mult, R=[ctk, "ones_bf"], W=[("n0rep", sc)],
                   eng="pool")
                stt(cx, cx, dec[:, h, u["dci"]:u["dci"] + 1], PS[cb_][:, 0:129], ALU.mult, ALU.add,
                    R=[ctk, ("dec", h), pst(cb_)], W=[ctk])
                if tt_i == 4:
                    out_ops.append(dma("sp", "ocst", [(cst_d[l].rearrange("p (g e) -> p g e", g=20)[:, sidx, :], cx)], R=[ctk]))
                elif tt_i == 3 and ui == len(units) - 1:
                    out_ops.append(dma("sp", "ocst", [(cst_d[l].rearrange("p (g e) -> p g e", g=20)[:, h, :], cx)], R=[ctk]))
                if u["first"]:
                    sw = swT4[P0:P0 + 64, st, h, :]
                    mm(PS[NUMB][:, cc:cc + 64], v_tok[P0:P0 + 64, st, h, 0:128], sw, True, False,
                       R=["v_tok", "swT4"], W=[pst(NUMB)])
                    mm(PS[DENB][:, cc:cc + 64], ones_bf[P0:P0 + 64, :], sw, True, False,
                       R=["ones_bf", "swT4"], W=[pst(DENB)])
                if ui + 1 < len(units):
                    emit_cupd(ui + 1, h)
                gc, gn = u["gc"], u["gn"]
                mm(PS[NUMB][:, gc:gc + gn], Cbf[sc], qg[:, gc:gc + gn], False, u["last"],
                   R=[("Cbf", sc), qtk], W=[pst(NUMB)])
                mm(PS[DENB][:, gc:gc + gn], n0rep[sc], qg[:, gc:gc + gn], False, u["last"],
                   R=[("n0rep", sc), qtk], W=[pst(DENB)])

            def head_post(h, sset):
                NUMB, DENB = NUMBs[h % 2], DENBs[h % 2]
                if sset == 0:
                    numS, Dm, tot, t1 = pp
                    tk = ["pp0", "pp1", "pp2", "pp3"]
                    sq, sqk = sbs[2], "sb2"
                else:
                    numS, Dm, tot, t1 = pq
                    tk = [("pq", i) for i in range(4)]
                    sq, sqk = sbs[3], "sb3"
                act(sq[:, 0:n], PS[NUMB][:, 0:n], AF.Square, R=[pst(NUMB)], W=[sqk])
                yield
                act(Dm[:, 0:n], PS[DENB][:, 0:n], AF.Abs, R=[pst(DENB)], W=[tk[1]])
                b = bigbank()
                mm(PS[b][:, 0:n], SELb[:, h, :], rEb[:, 0:n], True, True, R=["rEb", "SELb"], W=[pst(b)])
                b2 = bigbank()
                mm(PS[b2][:, 0:n], ones_bf, sq[:, 0:n], True, True, R=[sqk, "ones_bf"], W=[pst(b2)])
                yield
                tt(Dm[:, 0:n], Dm[:, 0:n], PS[b][:, 0:n], ALU.max, R=[tk[1], pst(b)], W=[tk[1]])
                yield
                stt(tot[:, 0:n], Dm[:, 0:n], 128.0 * EPS, Dm[:, 0:n], ALU.mult, ALU.mult, R=[tk[1]], W=[tk[2]])
                yield
                tt(tot[:, 0:n], tot[:, 0:n], PS[b2][:, 0:n], ALU.add, R=[tk[2], pst(b2)], W=[tk[2]])
                yield
                act(tot[:, 0:n], tot[:, 0:n], AF.Ln, scale=1.0 / 128, R=[tk[2]], W=[tk[2]])
                yield
                act(tot[:, 0:n], tot[:, 0:n], AF.Exp, scale=-0.5, R=[tk[2]], W=[tk[2]])
                yield
                tt(t1[:, 0:n], PS[NUMB][:, 0:n], tot[:, 0:n], ALU.mult, R=[pst(NUMB), tk[2]], W=[tk[3]])
                yield
                stt(mixT[:, h, 0:n], t1[:, 0:n], pfc(l, 64, h), sigo[:, h, 0:n], ALU.mult, ALU.mult,
                    R=[tk[3], "pf", "sigo"], W=["mixT"])

            pq_tok = [("pq", i) for i in range(4)]
            S.add("dve", lambda e: e.memset(dummy[:, 0:1], 0.0), W=["hT", "dummy"] + pq_tok)
            for hp in (0, 2):
                emit_cupd(0, hp)
                emit_cupd(0, hp + 1)
                for ui in range(len(units)):
                    emit_unit(ui, hp)
                    emit_unit(ui, hp + 1)
                bigpool[0] = [0, 1, 2, 7]
                gens = [head_post(hp, 0), head_post(hp + 1, 1)]
                while gens:
                    for g_ in list(gens):
                        try:
                            next(g_)
                        except StopIteration:
                            gens.remove(g_)
                bigpool[0] = [0, 1]
            bigpool[0] = [0, 1, 2, 5, 6, 7, 3, 4]
            S.add("dve", lambda e: e.memset(dummy[:, 0:1], 0.0), W=["hT", "dummy"] + pq_tok)
            if tt_i + 1 < len(TILES):
                emit_norm1(tt_i + 1)

            for oc in range(8):
                slot = wout_slot[oc]
                b = bigbank()
                for kc in range(8):
                    mm(PS[b][:, 0:n], wout_ring[slot][:, kc, :], mixT[:, kc, 0:n], kc == 0, kc == 7,
                       R=[("wout", slot), "mixT"], W=[pst(b)])
                if oc + 3 < 8:
                    wout_load(oc + 3)
                for (so, sn, bb) in segs:
                    stt(xT[:, oc, c0 + so:c0 + so + sn], PS[b][:, so:so + sn], modS[p][:, 16 + oc, bb:bb + 1],
                        xT[:, oc, c0 + so:c0 + so + sn], ALU.mult, ALU.add, R=[pst(b), xt] + mtok, W=[xt])
        out_ops.append(dma("sp", "omout", [(mout_d[:, 5 * l:5 * l + 5], moutS[:, 5 * l:5 * l + 5])], R=["moutS"]))

    def phase_B(l):
        p = l % 2
        mtok = [("mod", p), ("modA", p)]
        wuv = w_up_d[l].rearrange("(k p) n -> p k n", p=128)
        wdv = w_down_d[l]
        NE = 8

        def load_wu(q):
            for fc in range(4):
                f0 = (q * 4 + fc) * 128
                dma("pool", "wu%d" % fc, [(wu[:, :, fc * 128:(fc + 1) * 128], wuv[:, :, f0:f0 + 128])],
                    R=PHASE, W=[("wu", fc)])

        def load_wd(q):
            for fc in range(4):
                f0 = (q * 4 + fc) * 128
                dma("pool", "wd%d" % fc, [(wd[:, fc, :], wdv[f0:f0 + 128, :])], R=PHASE, W=[("wd", fc)])

        bigpool[0] = [0, 1, 2, 3, 4, 5, 6]
        load_wu(0)
        load_wd(0)
        if l + 1 < nl:
            modB_load(l + 1, 0)
        def emit_norm2(tt_i):
            c0, n = TILES[tt_i]
            segs = segs_of(tt_i, n)
            xt = ("x", tt_i)
            rms_rstd(tt_i, c0, n, sqB, "sqB", pp[0], "pp0")
            for kc in range(8):
                tmp = pp[1 + kc % 2]
                tk = "pp%d" % (1 + kc % 2)
                tt(tmp[:, 0:n], xT[:, kc, c0:c0 + n], pp[0][:, 0:n], ALU.mult, R=[xt, "pp0"], W=[tk])
                for (so, sn, b) in segs:
                    if False:
                        ts(h2T[:, kc, c0 + so:c0 + so + sn], tmp[:, so:so + sn], A2[p][:, kc, b:b + 1],
                           modS[p][:, 24 + kc, b:b + 1], ALU.mult, ALU.add, R=[tk] + mtok + PHASE, W=[("h2T", tt_i)], eng="pool")
                    else:
                        act(h2T[:, kc, c0 + so:c0 + so + sn], tmp[:, so:so + sn], AF.Identity, bias=modS[p][:, 24 + kc, b:b + 1],
                            scale=A2[p][:, kc, b:b + 1], R=[tk] + mtok + PHASE, W=[("h2T", tt_i)])

        def emit_up(fc, tt_i, dve_relu=False):
            c0, n = TILES[tt_i]
            b = bigbank()
            for kc in range(8):
                mm(PS[b][:, 0:n], wu[:, kc, fc * 128:(fc + 1) * 128], h2T[:, kc, c0:c0 + n], kc == 0, kc == 7,
                   R=[("wu", fc), ("h2T", tt_i)], W=[pst(b)])
            r = sbs[(fc * 5 + tt_i) % 3]
            rk = "sb%d" % ((fc * 5 + tt_i) % 3)
            if dve_relu:
                ts(r[:, 0:n], PS[b][:, 0:n], 0.0, None, ALU.max, R=[pst(b)], W=[rk])
            else:
                act(r[:, 0:n], PS[b][:, 0:n], AF.Relu, R=[pst(b)], W=[rk])
            tt(actT[:, fc, c0:c0 + n], r[:, 0:n], r[:, 0:n], ALU.mult, R=[rk], W=[("act", fc, tt_i)])

        for q in range(NE):
            if q == 0:
                emit_norm2(0)
                for tt_i in range(len(TILES)):
                    if tt_i + 1 < len(TILES):
                        emit_norm2(tt_i + 1)
                    for fc in range(4):
                        emit_up(fc, tt_i, dve_relu=True)
            else:
                for fc in range(4):
                    for tt_i in range(len(TILES)):
                        emit_up(fc, tt_i)
            if q + 1 < NE:
                load_wu(q + 1)
            if l + 1 < nl:
                modB_compute(l + 1, q)
                if q + 1 < NE:
                    modB_load(l + 1, q + 1)
            for oc in range(8):
                for tt_i, (c0, n) in enumerate(TILES):
                    segs = segs_of(tt_i, n)
                    xt = ("x", tt_i)
                    b = bigbank()
                    for fc in range(4):
                        mm(PS[b][:, 0:n], wd[:, fc, oc * 128:(oc + 1) * 128], actT[:, fc, c0:c0 + n], fc == 0, fc == 3,
                           R=[("wd", fc), ("act", fc, tt_i)], W=[pst(b)])
                    for (so, sn, bb) in segs:
                        stt(xT[:, oc, c0 + so:c0 + so + sn], PS[b][:, so:so + sn], modS[p][:, 40 + oc, bb:bb + 1],
                            xT[:, oc, c0 + so:c0 + so + sn], ALU.mult, ALU.add, R=[pst(b), xt] + mtok, W=[xt])
            if q + 1 < NE:
                load_wd(q + 1)

    ALLTOK = (["hT", "mixT", "qT", "kT", "sigo", "k_tok", "v_tok", "swT4", "kw4", "uP", "uS", "bkv",
               ("Cbf", 0), ("Cbf", 1), ("n0rep", 0), ("n0rep", 1), ("Cbf", 2), ("Cbf", 3), ("n0rep", 2), ("n0rep", 3), "sqB", ("wadaB", 0), ("wadaB", 1), ("wadaB", 2), ("wadaB", 3), ("wadaB", 4), ("wadaB", 5), ("pq", 0), ("pq", 1), ("pq", 2), ("pq", 3), "dummy"] + [("cx", i) for i in range(20)]
              + [("win", s_) for s_ in range(nring)] + [("h2T", i) for i in range(5)]
              + [("act", f_, i) for f_ in range(4) for i in range(5)] + [("wu", f_) for f_ in range(4)]
              + [("wd", f_) for f_ in range(4)] + PHASE)
    emit_mod_blocks(0, 0, 48, use_win=True)
    for l in range(nl):
        phase_A(l)
        S.add("pool", lambda e: e.memset(Bcar[:, 0:1], 0.0), W=ALLTOK + ["Bcar"])
        phase_B(l)
        S.add("pool", lambda e: e.memset(Mcar[:, 0:1], 0.0), W=ALLTOK + ["Mcar"])
    for tt_i, (c0, n) in enumerate(TILES):
        rms_rstd(tt_i, c0, n, sqB, "sqB", pp[0], "pp0")
        for kc in range(8):
            yb = pp[1 + kc % 3]
            yk = "pp%d" % (1 + kc % 3)
            stt(yb[:, 0:n], xT[:, kc, c0:c0 + n], pf[:, NL * PFW + kc:NL * PFW + kc + 1], pp[0][:, 0:n], ALU.mult, ALU.mult,
                R=[("x", tt_i), "pp0", "pf"] + PHASE, W=[yk])
            out_ops.append(dma("sp", "oy%d" % (kc % 3), [(yT_d[:, kc, c0:c0 + n], yb[:, 0:n])], R=[yk]))
    fin = S.add("sp", lambda e: e.nop(), R=[], W=[], force=True)
    out_ops = [d for d in out_ops if id(d) not in S.dropped]
    fin.deps = list(out_ops)
    fin.dneed = {d.dkey: 16 * S.dma_count[d.dkey] for d in out_ops}
    if S.limit:
        lastop = {}
        for o in S.ops:
            if o is not fin and o.dkey is None:
                lastop[o.eng] = o
        fin.deps += list(lastop.values())
        alld = {}
        for o in S.ops:
            if o.dkey is not None:
                alld[o.dkey] = o
        fin.deps += list(alld.values())
        for o in alld.values():
            fin.dneed[o.dkey] = 16 * S.dma_count[o.dkey]
    S.emit(nc, es)
    es.close()
    return nc


_NC_CACHE = {}


def _get_nc():
    if "nc" not in _NC_CACHE:
        _NC_CACHE["nc"] = build_nc()
    return _NC_CACHE["nc"]


def _prep(x_prompt, x_sample, c_prompt, c_sample, state_C, state_n, state_m, state_conv,
          w_ada, b_ada, g_norm1, w_in, b_in, conv_w, g_mix_out, w_out, g_norm2, w_up, w_down, g_final):
    f = np.float32
    x_prompt = np.asarray(x_prompt, f); x_sample = np.asarray(x_sample, f)
    c_prompt = np.asarray(c_prompt, f); c_sample = np.asarray(c_sample, f)
    state_C = np.asarray(state_C, f); state_n = np.asarray(state_n, f)
    state_m = np.asarray(state_m, f); state_conv = np.asarray(state_conv, f)
    w_ada = np.ascontiguousarray(np.asarray(w_ada, f)); w_in = np.ascontiguousarray(np.asarray(w_in, f))
    w_out = np.ascontiguousarray(np.asarray(w_out, f)); w_up = np.ascontiguousarray(np.asarray(w_up, f))
    w_down = np.ascontiguousarray(np.asarray(w_down, f))
    b_ada = np.asarray(b_ada, f); g_norm1 = np.asarray(g_norm1, f); b_in = np.asarray(b_in, f)
    conv_w = np.asarray(conv_w, f); g_mix_out = np.asarray(g_mix_out, f); g_norm2 = np.asarray(g_norm2, f)
    g_final = np.asarray(g_final, f)

    def fm(v, nchunk):
        return v.reshape(nchunk, 128).T

    pf = np.zeros((128, NL * PFW + 8), f)
    for l in range(NL):
        o = PFW * l
        pf[:, o:o + 48] = fm(b_ada[l], 48)
        pf[:, o + 48:o + 56] = fm(g_norm1[l], 8)
        pf[:, o + 56:o + 64] = fm(g_norm2[l], 8)
        pf[:, o + 64:o + 72] = fm(g_mix_out[l], 8)
        bsel = np.concatenate([b_in[l, 0:2048], b_in[l, 2056:3592]])
        pf[:, o + 72:o + 100] = fm(bsel, 28)
        cw = conv_w[l].reshape(3, 4, 128)
        pf[:, o + 100:o + 112] = cw.transpose(2, 1, 0).reshape(128, 12)
    pf[:, NL * PFW:NL * PFW + 8] = fm(g_final, 8)
    bkv = np.ascontiguousarray(np.broadcast_to(b_in[:, None, 512:1536], (NL, 128, 1024))).astype(f)

    in_maps = []
    for i in range(8):
        xs = np.concatenate([x_prompt[i], x_sample[4 * i:4 * i + 4].reshape(64, D)], axis=0)
        xT = np.ascontiguousarray(xs.reshape(NT, 8, 128).transpose(2, 1, 0))
        cs = np.concatenate([c_prompt[i:i + 1], c_sample[4 * i:4 * i + 4]], axis=0)
        cT = np.ascontiguousarray(cs.reshape(5, 8, 128).transpose(2, 1, 0)).reshape(128, 40)
        pg = np.zeros((4, 24), f)
        for l in range(NL):
            pg[:, 2 * l] = b_in[l, 2048:2052]
            pg[:, 2 * l + 1] = b_in[l, 2052:2056]
            pg[:, 8 + 4 * l:8 + 4 * l + 4] = state_m[l, 4 * i:4 * i + 4, :].T
        sC = state_C[:, 4 * i:4 * i + 4]
        sn = state_n[:, 4 * i:4 * i + 4]
        cext = np.concatenate([sC, sn[..., None]], axis=-1)
        cst0 = np.ascontiguousarray(cext.transpose(0, 3, 1, 2, 4)).reshape(NL, 128, 16 * 129)
        sc = state_conv[:, 4 * i:4 * i + 4]
        sconv0 = np.ascontiguousarray(sc.reshape(NL, 4, 2, 4, 128).transpose(0, 4, 3, 1, 2)).reshape(NL, 128, 32)
        in_maps.append({"xT": xT, "cT": cT, "pf": pf, "pg": pg, "cst0": cst0, "sconv0": sconv0, "bkv": bkv,
                        "w_ada": w_ada, "w_in": w_in, "w_out": w_out, "w_up": w_up, "w_down": w_down})

    return in_maps


def _assemble(R):
    f = np.float32

    y_prompt = np.zeros((8, T, D), f); y_sample = np.zeros((32, 16, D), f)
    p_C = np.zeros((NL, 8, 4, 128, 128), f); p_n = np.zeros((NL, 8, 4, 128), f); p_m = np.zeros((NL, 8, 4), f)
    p_conv = np.zeros((NL, 8, 2, 512), f)
    s_C = np.zeros((NL, 32, 4, 128, 128), f); s_n = np.zeros((NL, 32, 4, 128), f); s_m = np.zeros((NL, 32, 4), f)
    s_conv = np.zeros((NL, 32, 2, 512), f)
    for i in range(8):
        r = R[i]
        y = np.asarray(r["yT"]).transpose(2, 1, 0).reshape(NT, D)
        y_prompt[i] = y[:T]
        y_sample[4 * i:4 * i + 4] = y[T:].reshape(4, 16, D)
        cst = np.asarray(r["cst"]).reshape(NL, 128, 20, 129)
        mo = np.asarray(r["mout"]).reshape(4, NL, 5)
        co = np.asarray(r["convout"]).reshape(NL, 128, 4, 5, 2)
        for l in range(NL):
            for h in range(4):
                p_C[l, i, h] = cst[l, :, h, :128]
                p_n[l, i, h] = cst[l, :, h, 128]
                p_m[l, i, h] = mo[h, l, 0]
                for s in range(4):
                    s_C[l, 4 * i + s, h] = cst[l, :, 4 + 4 * s + h, :128]
                    s_n[l, 4 * i + s, h] = cst[l, :, 4 + 4 * s + h, 128]
                    s_m[l, 4 * i + s, h] = mo[h, l, 1 + s]
            p_conv[l, i] = co[l, :, :, 0, :].transpose(2, 1, 0).reshape(2, 512)
            for s in range(4):
                s_conv[l, 4 * i + s] = co[l, :, :, 1 + s, :].transpose(2, 1, 0).reshape(2, 512)
    return (y_prompt, y_sample, p_C, p_n, p_m, p_conv, s_C, s_n, s_m, s_conv)


def kernel(**inputs):
    in_maps = _prep(**inputs)
    nc = _get_nc()
    res = run_bass_kernel_spmd(nc, in_maps, core_ids=list(range(8)))
    return _assemble(res.results)
```
